# Optimizing a Trainium2 kernel written in Bass

```python
import math
import jax, jax.numpy as jnp
from jax import lax
import numpy as np

D_MODEL = 1024
BATCH = 2
SEQ = 16384
DEPTH = 2

HEAD_DIM = 64
CONV_GROUPS = 6
GMLP_HEADS = 6
XATTN_HEADS = 4
CONV_W = CONV_GROUPS * HEAD_DIM
GMLP_W = GMLP_HEADS * HEAD_DIM
XATTN_W = XATTN_HEADS * HEAD_DIM
MIX_W = CONV_W + GMLP_W + XATTN_W
IN_W = 2 * CONV_W + 2 * GMLP_W + XATTN_W
CONV_K = 31
CHUNK = 128
N_MEM = 256
D_FF = 2752
FFN_CONV_K = 3
DEEPNORM_ALPHA = (2.0 * DEPTH) ** 0.25
DEEPNORM_BETA = (8.0 * DEPTH) ** -0.25
LN_EPS = 1e-5

kernel_name = "hybrid_conv_gmlp_xattn_deepnorm"


def _layernorm(x, g, b):
    xf = x.astype(jnp.float32)
    mu = jnp.mean(xf, axis=-1, keepdims=True)
    var = jnp.mean(jnp.square(xf - mu), axis=-1, keepdims=True)
    y = (xf - mu) * lax.rsqrt(var + LN_EPS)
    return (y * g.astype(jnp.float32) + b.astype(jnp.float32)).astype(x.dtype)


def _causal_dwconv(x, w, b):
    k, c = w.shape
    y = lax.conv_general_dilated(
        x, w[:, None, :].astype(x.dtype), window_strides=(1,), padding=[(k - 1, 0)],
        dimension_numbers=("NWC", "WIO", "NWC"), feature_group_count=c)
    return y + b.astype(x.dtype)


def _chunk_spatial_gate(u, v, w_s, b_s, ln_g, ln_b):
    bsz, seq, _ = v.shape
    v = _layernorm(v, ln_g, ln_b)
    vc = v.reshape(bsz, seq // CHUNK, CHUNK, GMLP_HEADS, HEAD_DIM)
    mask = jnp.tril(jnp.ones((CHUNK, CHUNK), dtype=bool))
    ws = jnp.where(mask[None], w_s, jnp.zeros((), w_s.dtype)).astype(v.dtype)
    mixed = jnp.einsum("hts,bnshd->bnthd", ws, vc) + b_s.T[:, :, None].astype(v.dtype)
    return u * mixed.reshape(bsz, seq, GMLP_W)


def _memory_cross_attention(q, mem, w_mk, w_mv):
    bsz, seq, _ = q.shape
    m = mem.shape[1]
    qh = q.reshape(bsz, seq, XATTN_HEADS, HEAD_DIM)
    kh = (mem @ w_mk).reshape(bsz, m, XATTN_HEADS, HEAD_DIM)
    vh = (mem @ w_mv).reshape(bsz, m, XATTN_HEADS, HEAD_DIM)
    s = jnp.einsum("bshd,bmhd->bhsm", qh, kh).astype(jnp.float32) * (1.0 / math.sqrt(HEAD_DIM))
    p = jax.nn.softmax(s, axis=-1).astype(vh.dtype)
    o = jnp.einsum("bhsm,bmhd->bshd", p, vh)
    return o.reshape(bsz, seq, XATTN_W)


def setup_inputs(seed: int = 0) -> dict:
    key = jax.random.key(seed)
    ks = jax.random.split(key, 24)
    n = jax.random.normal
    f32 = jnp.float32
    L = DEPTH
    return {
        "x": n(ks[0], (BATCH, SEQ, D_MODEL), f32),
        "mem": n(ks[1], (BATCH, N_MEM, D_MODEL), f32),
        "w_in": n(ks[2], (L, D_MODEL, IN_W), f32) * D_MODEL ** -0.5,
        "conv_a_w": n(ks[3], (L, CONV_K, CONV_W), f32) * CONV_K ** -0.5,
        "conv_a_b": n(ks[4], (L, CONV_W), f32) * 0.02,
        "ln_a_g": 1.0 + 0.05 * n(ks[5], (L, CONV_W), f32),
        "ln_a_b": 0.02 * n(ks[6], (L, CONV_W), f32),
        "ln_v_g": 1.0 + 0.05 * n(ks[7], (L, GMLP_W), f32),
        "ln_v_b": 0.02 * n(ks[8], (L, GMLP_W), f32),
        "w_s": n(ks[9], (L, GMLP_HEADS, CHUNK, CHUNK), f32) * CHUNK ** -0.5,
        "b_s": 1.0 + 0.05 * n(ks[10], (L, GMLP_HEADS, CHUNK), f32),
        "w_mk": n(ks[11], (L, D_MODEL, XATTN_W), f32) * D_MODEL ** -0.5,
        "w_mv": n(ks[12], (L, D_MODEL, XATTN_W), f32) * (D_MODEL ** -0.5 * DEEPNORM_BETA),
        "w_out": n(ks[13], (L, MIX_W, D_MODEL), f32) * (MIX_W ** -0.5 * DEEPNORM_BETA),
        "ln1_g": 1.0 + 0.05 * n(ks[14], (L, D_MODEL), f32),
        "ln1_b": 0.02 * n(ks[15], (L, D_MODEL), f32),
        "w_up": n(ks[16], (L, D_MODEL, 2 * D_FF), f32) * D_MODEL ** -0.5,
        "conv_f_w": n(ks[17], (L, FFN_CONV_K, D_FF), f32) * FFN_CONV_K ** -0.5,
        "conv_f_b": n(ks[18], (L, D_FF), f32) * 0.02,
        "w_down": n(ks[19], (L, D_FF, D_MODEL), f32) * (D_FF ** -0.5 * DEEPNORM_BETA),
        "ln2_g": 1.0 + 0.05 * n(ks[20], (L, D_MODEL), f32),
        "ln2_b": 0.02 * n(ks[21], (L, D_MODEL), f32),
    }


def reference(x, mem, w_in, conv_a_w, conv_a_b, ln_a_g, ln_a_b, ln_v_g, ln_v_b, w_s, b_s,
              w_mk, w_mv, w_out, ln1_g, ln1_b, w_up, conv_f_w, conv_f_b, w_down, ln2_g, ln2_b):
    o1 = CONV_W
    o2 = 2 * CONV_W
    o3 = o2 + GMLP_W
    o4 = o3 + GMLP_W
    for l in range(DEPTH):
        h = x @ w_in[l]
        a = h[..., :o1] * jax.nn.sigmoid(h[..., o1:o2])
        a = _causal_dwconv(a, conv_a_w[l], conv_a_b[l])
        a = jax.nn.silu(_layernorm(a, ln_a_g[l], ln_a_b[l]))
        u = jax.nn.gelu(h[..., o2:o3])
        v = jax.nn.gelu(h[..., o3:o4])
        g = _chunk_spatial_gate(u, v, w_s[l], b_s[l], ln_v_g[l], ln_v_b[l])
        c = _memory_cross_attention(h[..., o4:], mem, w_mk[l], w_mv[l])
        mix = jnp.concatenate([a, g, c], axis=-1) @ w_out[l]
        x = _layernorm(DEEPNORM_ALPHA * x + mix, ln1_g[l], ln1_b[l])
        up = x @ w_up[l]
        gate = _causal_dwconv(up[..., :D_FF], conv_f_w[l], conv_f_b[l])
        y = (jax.nn.silu(gate) * up[..., D_FF:]) @ w_down[l]
        x = _layernorm(DEEPNORM_ALPHA * x + y, ln2_g[l], ln2_b[l])
    return x
```

```python
import contextlib
import numpy as np
import concourse.bass as bass
import concourse.mybir as mybir
from concourse.bass_utils import run_bass_kernel_spmd

F32 = mybir.dt.float32
BF16 = mybir.dt.bfloat16
AF = mybir.ActivationFunctionType
ALU = mybir.AluOpType

D = 1024
SEQ = 16384
BATCH = 2
DEPTH = 2
NCORES = 8
TOK_PER_CORE = SEQ * BATCH // NCORES
CW = 384
XW = 256
INW = 1792
CK = 31
DFF = 2752
NJ = 22
NMEM = 256
ALPHA = (2.0 * DEPTH) ** 0.25
EPS = 1e-5
TW = 512
NSLOT = 5
PL = 93 + 9 + 32 + 66 + 22 + 6
O_CAW, O_CAB, O_LAG, O_LAB = 0, 93, 96, 99
O_L1G, O_L1B, O_L2G, O_L2B = 102, 110, 118, 126
O_CFW, O_CFB = 134, 200
O_LVG, O_LVB = 222, 225

SB_IN = 0
SB_DIAG = 6
SB_V = 18
SB_Q = 21
SB_U = 23
SB_OUT = 26
SB_UP = 34
SB_DOWN = 78
NSB = 100
NUNIT = 25

COMPUTE = ("pe", "act", "dve", "pool")
QUEUES = ("sp",)
SAME_ENG_DIST = 1 << 30


class Op:
    __slots__ = ("eng", "fn", "reads", "writes", "seq", "deps", "inc", "val",
                 "dma", "clock", "waits", "name", "cid", "ndma")

    def __init__(self, eng, fn, reads, writes, dma, name):
        self.eng = eng
        self.fn = fn
        self.reads = reads
        self.writes = writes
        self.dma = dma
        self.name = name
        self.deps = []
        self.inc = False
        self.val = 0
        self.waits = []
        self.clock = None
        self.cid = None
        self.ndma = 1


class Sched:
    def __init__(self, nc, stack):
        self.nc = nc
        self.stack = stack
        self.ops = []
        self.eng_ops = {e: [] for e in COMPUTE + QUEUES}
        self.sems = {}

    def op(self, eng, fn, reads=(), writes=(), dma=None, name=""):
        o = Op(eng, fn, tuple(reads), tuple(writes), dma, name)
        o.seq = len(self.eng_ops[eng]) + 1
        self.eng_ops[eng].append(o)
        self.ops.append(o)
        return o

    def pe(self, fn, reads=(), writes=(), name=""):
        return self.op("pe", fn, reads, writes, None, name)

    def act(self, fn, reads=(), writes=(), name=""):
        return self.op("act", fn, reads, writes, None, name)

    def dve(self, fn, reads=(), writes=(), name=""):
        return self.op("dve", fn, reads, writes, None, name)

    def pool(self, fn, reads=(), writes=(), name=""):
        return self.op("pool", fn, reads, writes, None, name)

    def dma(self, fn, sem, reads=(), writes=(), eng="sp", name="", n=1):
        o = self.op(eng, fn, reads, writes, ("dma", sem), name)
        o.ndma = n
        return o

    def _analyze(self):
        last_w = {}
        readers = {}
        last_dma_on_sem = {}
        for o in self.ops:
            deps = {}
            for b in o.reads:
                w = last_w.get(b)
                if w is not None:
                    deps[id(w)] = w
            for b in o.writes:
                w = last_w.get(b)
                if w is not None:
                    deps[id(w)] = w
                for r in readers.get(b, ()):
                    deps[id(r)] = r
            if o.dma is not None:
                p = last_dma_on_sem.get(o.dma)
                if p is not None:
                    deps[id(p)] = p
                last_dma_on_sem[o.dma] = o
            deps.pop(id(o), None)
            o.deps = list(deps.values())
            for b in o.reads:
                readers.setdefault(b, []).append(o)
            for b in o.writes:
                last_w[b] = o
                readers[b] = []
        for o in self.ops:
            o.cid = o.dma if o.dma is not None else o.eng
        eng_clock = {e: {} for e in self.eng_ops}
        ordidx = {}
        per_cid_seq = {}
        for o in self.ops:
            n = per_cid_seq.get(o.cid, 0) + 1
            per_cid_seq[o.cid] = n
            ordidx[id(o)] = n
        for o in self.ops:
            clk = eng_clock[o.eng]
            best = {}
            for d in o.deps:
                dn = ordidx[id(d)]
                if d.dma is None and d.eng == o.eng:
                    if o.eng == "pe":
                        continue
                    if o.dma is None and o.seq - d.seq > SAME_ENG_DIST:
                        continue
                if clk.get(d.cid, 0) >= dn:
                    continue
                if d.cid not in best or dn > ordidx[id(best[d.cid])]:
                    best[d.cid] = d
            final = []
            for d in sorted(best.values(), key=lambda d: -len(d.clock)):
                if clk.get(d.cid, 0) >= ordidx[id(d)]:
                    continue
                final.append(d)
                for k, v in d.clock.items():
                    if clk.get(k, 0) < v:
                        clk[k] = v
                clk[d.cid] = max(clk.get(d.cid, 0), ordidx[id(d)])
            for d in final:
                d.inc = True
            o.waits = final
            o.clock = dict(clk)
        cnt = {}
        for o in self.ops:
            if o.dma is not None:
                cnt[o.cid] = cnt.get(o.cid, 0) + 16 * o.ndma
                o.val = cnt[o.cid]
                o.inc = True
            elif o.inc:
                cnt[o.cid] = cnt.get(o.cid, 0) + 1
                o.val = cnt[o.cid]

    def finish(self, final_waits=()):
        nc = self.nc
        self._analyze()
        cids = set(o.cid for o in self.ops if o.inc)
        for cid in sorted(cids, key=str):
            nm = "s_" + (cid if isinstance(cid, str) else "d_" + str(cid[1]))
            self.sems[cid] = self.stack.enter_context(nc.semaphore(nm))
        block = self.stack.enter_context(nc.Block())
        sems = self.sems

        def emit(engobj, ename):
            for o in self.eng_ops[ename]:
                for d in o.waits:
                    engobj.wait_ge(sems[d.cid], d.val)
                ins = o.fn(engobj)
                if o.dma is not None:
                    if not isinstance(ins, (list, tuple)):
                        ins = [ins]
                    assert len(ins) == o.ndma, o.name
                    for i_ in ins:
                        i_.then_inc(sems[o.cid], 16)
                elif o.inc:
                    assert ins is not None, o.name
                    ins.then_inc(sems[o.cid], 1)
            if ename == "sp":
                for d in final_waits:
                    engobj.wait_ge(sems[d.cid], d.val)

        @block.tensor
        def _(e):
            emit(e, "pe")

        @block.scalar
        def _(e):
            emit(e, "act")

        @block.vector
        def _(e):
            emit(e, "dve")

        @block.gpsimd
        def _(e):
            emit(e, "pool")

        @block.sync
        def _(e):
            emit(e, "sp")


def build(L, H, n_main=8, last_out=True):
    NT = H + n_main * TW
    NCOL = L * PL + 1
    MASKC = L * PL
    nc = bass.Bass("TRN2", target_bir_lowering=False)

    def din(name, shape, dt=F32):
        return nc.dram_tensor(name, list(shape), dt, kind="ExternalInput").ap()

    xT = din("xT", [D, NT])
    memT = din("memT", [D, NMEM])
    pcols_d = din("pcols", [128, NCOL])
    ident_d = din("ident", [128, 128])
    triu_d = din("triu", [128, 128])
    bsb_d = din("bsb", [128, L * 3 * 128])
    wsT_d = din("wsT", [L, 6, 128, 128])
    w_in = din("w_in", [L, D, INW])
    w_out = din("w_out", [L, D, D])
    w_up = din("w_up", [L, D, 2 * DFF])
    w_down = din("w_down", [L, DFF, D])
    w_mk = din("w_mk", [L, D, XW])
    w_mv = din("w_mv", [L, D, XW])
    outT = nc.dram_tensor("outT", [D, n_main * TW], F32, kind="ExternalOutput").ap()
    wst = nc.dram_tensor("wst", [L, NUNIT, 128, 4096], BF16, kind="Internal").ap()

    st = contextlib.ExitStack()
    with st:
        S = Sched(nc, st)

        def sb(name, shape, dt):
            return st.enter_context(nc.sbuf_tensor(name, list(shape), dt))

        pcols = sb("pcols_s", [128, NCOL], F32)
        ident = sb("ident_s", [128, 128], F32)
        ones_bf = sb("ones_bf", [128, 128], BF16)
        eones = sb("eones", [128, 2, 128], BF16)
        bsb = sb("bsb_s", [128, L, 3, 128], F32)
        wsT = sb("wsT_s", [128, L, 6, 128], BF16)
        kT = sb("kT", [128, L, 2, NMEM], BF16)
        vpad = sb("vpad", [128, L, 2, 4, 128], BF16)
        abuf = sb("abuf", [128, L, 3, 30 + TW], BF16)
        gst = sb("gst", [128, L, NJ, 2], F32)
        ring = sb("ring", [128, NSLOT, 4096], BF16)
        xf = sb("xf", [128, 8, TW], F32)
        xb = sb("xb", [128, 8, TW], BF16)
        sg = sb("sg", [128, 2, TW], F32)
        ub = sb("ub", [128, 3, TW], F32)
        vg = sb("vg", [128, 4, 3, 2, 64], F32)
        vst = sb("vst", [128, 4, 6], F32)
        vmv = sb("vmv", [128, 4, 2], F32)
        vrs = sb("vrs", [128, 4], F32)
        vpadb = sb("vpadb", [128, 4, 3, 2, 128], BF16)
        qT = sb("qT", [128, 2, TW], BF16)
        rs = sb("rs", [128, 2, TW], F32)
        cat = sb("cat", [128, 8, TW], BF16)
        yc = sb("yc", [128, 3, TW], F32)
        zb = sb("zb", [128, 4, TW], BF16)
        zq = sb("zq", [128, 4, TW], BF16)
        lmsq = sb("lmsq", [128, TW], F32)
        lrstd = sb("lrstd", [128, TW], F32)
        lnmr = sb("lnmr", [128, TW], F32)
        lt = sb("lt", [128, 2, TW], F32)
        ft1 = sb("ft1", [128, 2, TW], F32)
        gb = sb("gb", [128, 2, TW + 2], F32)
        facc = sb("facc", [128, 2, TW], F32)
        fs = sb("fs", [128, 2, TW], F32)
        hid = sb("hid", [128, NJ, TW], BF16)
        t1g = sb("t1g", [128, 2, TW], F32)
        scr = sb("scr", [128, 8], F32)

        psum = [st.enter_context(nc.psum_tensor("ps%d" % i, [128, TW], F32)) for i in range(8)]
        free_banks = list(range(8))

        def balloc():
            return free_banks.pop(0)

        def bfree(b):
            free_banks.append(b)

        def PK(b):
            return ("ps", b)

        def col(l, off, i=0):
            c = l * PL + off + i
            return pcols[:, c:c + 1]

        def ld(dst, src, sem, key, eng="sp"):
            return S.dma(lambda e: e.dma_start(out=dst, in_=src), sem, writes=[key], eng=eng)

        ld(pcols[:], pcols_d, "c0", "pcols")
        ld(ident[:], ident_d, "c1", "ident")
        ld(bsb[:].rearrange("p l j t -> p (l j t)"), bsb_d, "c4", "bsb")
        S.pool(lambda e: e.memset(ones_bf[:], 1.0), writes=["ones"])
        S.pool(lambda e: e.memset(eones[:], 0.0), writes=["eones"])
        S.pool(lambda e: e.memset(eones[:, 0, 0:64], 1.0), reads=[], writes=["eones"])
        S.pool(lambda e: e.memset(eones[:, 1, 64:128], 1.0), reads=[], writes=["eones"])
        S.pool(lambda e: e.memset(vpadb[:].rearrange("p a b c d -> p (a b c d)"), 0.0), writes=["vpadb"])
        S.pool(lambda e: e.memset(vpad[:].rearrange("p a b c d -> p (a b c d)"), 0.0), writes=["vpad"])
        S.pool(lambda e: e.memset(abuf[:].rearrange("p a b c -> p (a b c)"), 0.0),
               writes=[("abuf", l, c) for l in range(L) for c in range(3)])
        S.pool(lambda e: e.memset(gst[:].rearrange("p a b c -> p (a b c)"), 0.0),
               writes=[("gst", l) for l in range(L)])

        NST = 4

        def st32(i):
            return xf[:, 2 * i:2 * i + 2, :].rearrange("p a b -> p (a b)")

        def st32k(i):
            return [("xf", 2 * i), ("xf", 2 * i + 1)]

        def st16(i):
            return hid[:, 2 * i:2 * i + 2, :].rearrange("p a b -> p (a b)")

        def st16k(i):
            return [("hid", 2 * i), ("hid", 2 * i + 1)]

        cast_rr = [0]

        def cast_op(dst, src, reads, writes):
            k = cast_rr[0] % 3
            cast_rr[0] += 1
            if k == 0:
                S.act(lambda e: e.activation(out=dst, in_=src, func=AF.Copy), reads=reads, writes=writes)
            elif k == 1:
                S.dve(lambda e: e.tensor_copy(out=dst, in_=src), reads=reads, writes=writes)
            else:
                S.pool(lambda e: e.tensor_copy(out=dst, in_=src), reads=reads, writes=writes)

        ld(st32(0)[:, 0:128], triu_d, "c5", ("xf", 0))
        for l in range(L):
            for h2 in range(2):
                src = wsT_d[l, 3 * h2:3 * h2 + 3].rearrange("h s t -> s h t")
                dst = st32(1)[:, 0:384].rearrange("p (h t) -> p h t", h=3)
                S.dma(lambda e, dst=dst, src=src: e.dma_start(out=dst, in_=src), "pin1",
                      writes=st32k(1))
                for hh in range(3):
                    h = 3 * h2 + hh
                    S.dve(lambda e, l=l, h=h, hh=hh: e.tensor_tensor(
                        out=wsT[:, l, h, :], in0=st32(1)[:, hh * 128:(hh + 1) * 128],
                        in1=st32(0)[:, 0:128], op=ALU.mult),
                        reads=st32k(1) + [("xf", 0)], writes=["wsT"])

        memTb = hid[:, 8:12, :].rearrange("p a b -> p (a b)").rearrange("p (k m) -> p k m", k=8)
        memk = [("hid", j) for j in range(8, 12)]
        wkvb = hid[:, 12:16, :].rearrange("p a b -> p (a b)").rearrange("p (k m) -> p k m", k=8)
        wkvk = [("hid", j) for j in range(12, 16)]

        def load_kv(dst, dkeys, src2d):
            for half in range(2):
                slot = 2 + half
                src = src2d[half * 512:(half + 1) * 512, :].rearrange("(k p) m -> p k m", p=128)
                dstage = st32(slot).rearrange("p (k m) -> p k m", k=4)
                S.dma(lambda e, d=dstage, s=src: e.dma_start(out=d, in_=s), "pin%d" % slot,
                      writes=st32k(slot))
                cast_op(dst[:, half * 4:(half + 1) * 4, :], dstage, st32k(slot), dkeys)

        load_kv(memTb, memk, memT)
        for l in range(L):
            load_kv(wkvb, wkvk, w_mk[l])
            for c2 in range(2):
                b = balloc()

                def fn(e, l=l, c2=c2, b=b):
                    for kc in range(8):
                        ins = e.matmul(psum[b][:, 0:NMEM], lhsT=wkvb[:, kc, c2 * 128:(c2 + 1) * 128],
                                       rhs=memTb[:, kc, :], start=(kc == 0), stop=(kc == 7))
                    return ins
                S.pe(fn, reads=memk + wkvk, writes=[PK(b)])
                S.act(lambda e, l=l, c2=c2, b=b: e.activation(out=kT[:, l, c2, :], in_=psum[b][:, 0:NMEM], func=AF.Copy),
                      reads=[PK(b)], writes=["kT"])
                bfree(b)
            load_kv(wkvb, wkvk, w_mv[l])
            for mc in range(2):
                b = balloc()

                def fn(e, l=l, mc=mc, b=b):
                    for kc in range(8):
                        ins = e.matmul(psum[b][:, 0:XW], lhsT=memTb[:, kc, mc * 128:(mc + 1) * 128],
                                       rhs=wkvb[:, kc, :], start=(kc == 0), stop=(kc == 7))
                    return ins
                S.pe(fn, reads=memk + wkvk, writes=[PK(b)])
                for h in range(4):
                    r = h % 2
                    S.dve(lambda e, l=l, mc=mc, h=h, r=r, b=b: e.tensor_copy(
                        out=vpad[:, l, mc, h, r * 64:(r + 1) * 64], in_=psum[b][:, h * 64:(h + 1) * 64]),
                        reads=[PK(b)], writes=["vpad"])
                bfree(b)

        for l in range(L):
            for j in range(3):
                b = balloc()

                def fn(e, l=l, j=j, b=b):
                    e.matmul(psum[b][:, 0:128], lhsT=eones[:, 0, :], rhs=wsT[:, l, 2 * j, :], start=True, stop=False)
                    return e.matmul(psum[b][:, 0:128], lhsT=eones[:, 1, :], rhs=wsT[:, l, 2 * j + 1, :],
                                    start=False, stop=True)
                S.pe(fn, reads=["eones", "wsT"], writes=[PK(b)])
                S.dve(lambda e, l=l, j=j, b=b: e.scalar_tensor_tensor(
                    out=bsb[:, l, j, :], in0=psum[b][:, 0:128], scalar=col(l, O_LVB, j), in1=bsb[:, l, j, :],
                    op0=ALU.mult, op1=ALU.add), reads=[PK(b), "bsb", "pcols"], writes=["bsb"])
                bfree(b)

        stg = sb("stg", [128, 4, 1024], F32)
        ring_state = {"count": 0, "loaded": {}, "produced": [], "stg": 0, "rr": 0, "res": {}}
        in_cols = {0: CW, 1: 0, 2: CW + 128, 3: 128, 4: CW + 256, 5: 256}
        for i_ in range(3):
            in_cols[SB_V + i_] = 1152 + 128 * i_
            in_cols[SB_U + i_] = 768 + 128 * i_
        for i_ in range(2):
            in_cols[SB_Q + i_] = 1536 + 128 * i_

        def kc_view(src2d, c0, cw):
            return src2d.rearrange("(kc p) n -> p kc n", p=128)[:, :, c0:c0 + cw]

        def sub_source(l, s):
            kv = lambda d: d.rearrange("p (k c) -> p k c", k=8)
            if s in in_cols:
                return [(kv, kc_view(w_in[l], in_cols[s], 128))], False
            if SB_OUT <= s < SB_UP:
                return [(kv, kc_view(w_out[l], (s - SB_OUT) * 128, 128))], False
            if SB_UP <= s < SB_DOWN:
                j, gv = (s - SB_UP) // 2, (s - SB_UP) % 2
                cw = 128 if j < NJ - 1 else 64
                return [(lambda d, cw=cw: kv(d)[:, :, 0:cw], kc_view(w_up[l], gv * DFF + j * 128, cw))], cw < 128
            hf, jj = (s - SB_DOWN) // 11, (s - SB_DOWN) % 11
            if jj < 10:
                src = w_down[l][jj * 256:(jj + 1) * 256, hf * 512:(hf + 1) * 512].rearrange("(jl p) c -> p jl c", p=128)
                return [(lambda d: d.rearrange("p (a c) -> p a c", a=2), src)], False
            s0 = w_down[l][2560:2688, hf * 512:(hf + 1) * 512]
            s1 = w_down[l][2688:2752, hf * 512:(hf + 1) * 512]
            return [(lambda d: d[:, 0:512], s0), (lambda d: d[0:64, 512:1024], s1)], True

        def cast2(dst, src, reads, writes):
            k = ring_state["rr"] % 2
            ring_state["rr"] += 1
            if k == 0:
                S.dve(lambda e: e.tensor_copy(out=dst, in_=src), reads=reads, writes=writes)
            else:
                S.act(lambda e: e.activation(out=dst, in_=src, func=AF.Copy), reads=reads, writes=writes)

        def produce_unit(l, unit):
            slot = ring_state["count"] % NSLOT
            ring_state["count"] += 1
            for q in range(4):
                s = 4 * unit + q
                dst16 = ring[:, slot, q * 1024:(q + 1) * 1024]
                rk = ("ring", slot, q)
                if SB_DIAG <= s < SB_V:
                    first = (s - SB_DIAG) * 8
                    on_act = (s % 2 == 1)

                    def fn(e, dst16=dst16, first=first, l=l, on_act=on_act):
                        for qq in range(8):
                            idx = first + qq
                            o = dst16[:, qq * 128:(qq + 1) * 128]
                            if idx >= 93:
                                ins = e.memset(o, 0.0) if not on_act else e.activation(
                                    out=o, in_=ident[:], func=AF.Copy, scale=0.0)
                            elif on_act:
                                ins = e.activation(out=o, in_=ident[:], func=AF.Identity, bias=0.0,
                                                   scale=col(l, O_CAW, idx))
                            else:
                                ins = e.tensor_scalar(out=o, in0=ident[:], scalar1=col(l, O_CAW, idx),
                                                      scalar2=None, op0=ALU.mult)
                        return ins
                    (S.act if on_act else S.dve)(fn, reads=["ident", "pcols"], writes=[rk])
                else:
                    k = ring_state["stg"] % 4
                    ring_state["stg"] += 1
                    d32 = stg[:, k, :]
                    sk = ("stg", k)
                    parts, zero_first = sub_source(l, s)
                    if zero_first:
                        S.pool(lambda e, d32=d32: e.memset(d32, 0.0), writes=[sk])
                    S.dma(lambda e, parts=parts, d32=d32: [e.dma_start(out=vf(d32), in_=src) for (vf, src) in parts],
                          "pin%d" % k, writes=[sk], n=len(parts))
                    cast2(dst16, d32, [sk], [rk])
            S.dma(lambda e, slot=slot, l=l, unit=unit: e.dma_start(out=wst[l, unit], in_=ring[:, slot, :]),
                  "pout%d" % (slot % 2), reads=[("ring", slot, q) for q in range(4)],
                  writes=[("wst", l, unit)], eng="pool")
            ring_state["loaded"][(0, l, unit)] = slot
            ring_state["res"][slot] = (0, l, unit)

        def ensure_produced(l, unit):
            order = [(ll, uu) for ll in range(L) for uu in range(NUNIT)]
            tgt = order.index((l, unit))
            while len(ring_state["produced"]) <= tgt:
                ll, uu = order[len(ring_state["produced"])]
                produce_unit(ll, uu)
                ring_state["produced"].append((ll, uu))

        def W(pkey, l, s):
            ti_ = pkey[0]
            unit = s // 4
            key = (ti_, l, unit)
            if ti_ == 0:
                order_idx = l * NUNIT + unit
                nxt = min(order_idx + 1, L * NUNIT - 1)
                ensure_produced(nxt // NUNIT, nxt % NUNIT)
            elif key not in ring_state["loaded"]:
                slot = ring_state["count"] % NSLOT
                ring_state["count"] += 1
                S.dma(lambda e, slot=slot, l=l, unit=unit: e.dma_start(out=ring[:, slot, :], in_=wst[l, unit]),
                      "ring%d" % slot, reads=[("wst", l, unit)],
                      writes=[("ring", slot, q) for q in range(4)])
                ring_state["loaded"][key] = slot
                ring_state["res"][slot] = key
            slot = ring_state["loaded"][key]
            assert ring_state["res"][slot] == key, ("ring slot evicted before use", key, ring_state["res"][slot])
            o = (s % 4) * 1024
            return ring[:, slot, o:o + 1024], ("ring", slot, s % 4)

        def ln_begin(n, T, lag):
            return {"b": None, "n": n, "T": T, "lag": lag, "q": [], "fed": 0}

        def _ln_mm(st):
            c, r = st["q"].pop(0)
            if st["b"] is None:
                st["b"] = (balloc(), balloc())
            b1, b2 = st["b"]
            n, T = st["n"], st["T"]
            S.pe(lambda e: e.matmul(psum[b1][:, 0:T], lhsT=ones_bf[:], rhs=zb[:, r, 0:T],
                                    start=(c == 0), stop=(c == n - 1)),
                 reads=["ones", ("zb", r)], writes=[PK(b1)])
            S.pe(lambda e: e.matmul(psum[b2][:, 0:T], lhsT=ones_bf[:], rhs=zq[:, r, 0:T],
                                    start=(c == 0), stop=(c == n - 1)),
                 reads=["ones", ("zq", r)], writes=[PK(b2)])

        def ln_feed(st, zc, zkey):
            c = st["fed"]
            st["fed"] += 1
            r = c % 4
            T = st["T"]
            S.act(lambda e: e.activation(out=zb[:, r, 0:T], in_=zc, func=AF.Copy), reads=[zkey], writes=[("zb", r)])
            S.act(lambda e: e.activation(out=zq[:, r, 0:T], in_=zc, func=AF.Square), reads=[zkey], writes=[("zq", r)])
            st["q"].append((c, r))
            while len(st["q"]) > st["lag"]:
                _ln_mm(st)

        def ln_flush(st):
            while st["q"]:
                _ln_mm(st)
            return st["b"]

        def ln_stats(z, zkeys, n, T):
            st = ln_begin(n, T, 0)
            for c in range(n):
                ln_feed(st, z[c], zkeys[c])
            return ln_flush(st)

        def ln_apply(stt, z, zkeys, n, Dn, T, gcols, bcols, func, outs):
            b1, b2 = stt
            inv = 1.0 / Dn
            S.act(lambda e: e.activation(out=lmsq[:, 0:T], in_=psum[b1][:, 0:T], func=AF.Square, scale=inv),
                  reads=[PK(b1)], writes=["lmsq"])
            S.dve(lambda e: e.scalar_tensor_tensor(out=lrstd[:, 0:T], in0=psum[b2][:, 0:T], scalar=inv,
                                                   in1=lmsq[:, 0:T], op0=ALU.mult, op1=ALU.subtract),
                  reads=[PK(b2), "lmsq"], writes=["lrstd"])
            S.act(lambda e: e.activation(out=lrstd[:, 0:T], in_=lrstd[:, 0:T], func=AF.Sqrt, bias=EPS, scale=1.0),
                  reads=["lrstd"], writes=["lrstd"])
            S.dve(lambda e: e.reciprocal(out=lrstd[:, 0:T], in_=lrstd[:, 0:T]), reads=["lrstd"], writes=["lrstd"])
            S.dve(lambda e: e.scalar_tensor_tensor(out=lnmr[:, 0:T], in0=psum[b1][:, 0:T], scalar=-inv,
                                                   in1=lrstd[:, 0:T], op0=ALU.mult, op1=ALU.mult),
                  reads=[PK(b1), "lrstd"], writes=["lnmr"])
            bfree(b1)
            bfree(b2)
            for c in range(n):
                r = c % 2
                S.dve(lambda e, c=c, r=r: e.tensor_tensor(out=lt[:, r, 0:T], in0=z[c], in1=lrstd[:, 0:T], op=ALU.mult),
                      reads=[zkeys[c], "lrstd"], writes=[("lt", r)])
                (S.pool if c % 2 == 0 else S.dve)(
                    lambda e, r=r: e.tensor_tensor(out=lt[:, r, 0:T], in0=lt[:, r, 0:T], in1=lnmr[:, 0:T], op=ALU.add),
                    reads=[("lt", r), "lnmr"], writes=[("lt", r)])
                for (oap, okey) in outs[c]:
                    S.act(lambda e, c=c, r=r, oap=oap: e.activation(out=oap, in_=lt[:, r, 0:T], func=func,
                                                                      bias=bcols[c], scale=gcols[c]),
                          reads=[("lt", r), "pcols"], writes=[okey])

        def ln_fm(z, zkeys, n, Dn, T, gcols, bcols, func, outs):
            stt = ln_stats(z, zkeys, n, T)
            ln_apply(stt, z, zkeys, n, Dn, T, gcols, bcols, func, outs)

        NT_ = H + n_main * TW
        tiles = []
        c_ = 0
        while c_ < NT_:
            w_ = min(TW, NT_ - c_)
            tiles.append((c_, w_))
            c_ += w_
        assert H < TW and tiles[0][1] == TW
        out_ops = []
        pending = [None]

        def load_xb(ti_):
            c0_, T_ = tiles[ti_]
            src = xT[:, c0_:c0_ + T_].rearrange("(kc p) t -> p kc t", p=128)
            S.dma(lambda e: e.dma_start(out=xb[:, :, 0:T_], in_=src), "xlb",
                  writes=[("xb", k) for k in range(8)], eng="pool")

        def load_xf(ti_):
            c0_, T_ = tiles[ti_]
            src = xT[:, c0_:c0_ + T_].rearrange("(kc p) t -> p kc t", p=128)
            S.dma(lambda e: e.dma_start(out=xf[:, :, 0:T_], in_=src), "xld",
                  writes=[("xf", k) for k in range(8)], eng="act")
        for ti, (c0, T) in enumerate(tiles):
            nt = T // 128
            if ti == 0:
                load_xb(0)
                load_xf(0)
            for l in range(L):
                pkey = (ti, l)
                xbk = [("xb", k) for k in range(8)]

                def proj(s, T=T, pkey=pkey, l=l, xbk=xbk):
                    wap, wkey = W(pkey, l, s)
                    b = balloc()

                    def fn(e):
                        for kc in range(8):
                            ins = e.matmul(psum[b][:, 0:T], lhsT=wap[:, kc * 128:(kc + 1) * 128],
                                           rhs=xb[:, kc, 0:T], start=(kc == 0), stop=(kc == 7))
                        return ins
                    S.pe(fn, reads=[wkey] + xbk, writes=[PK(b)])
                    return b

                def proj_multi(subs, T=T, pkey=pkey, l=l):
                    ws_ = [W(pkey, l, s_) for s_ in subs]
                    bs_ = [balloc() for _ in subs]
                    for kc in range(8):
                        def fn(e, kc=kc):
                            for (wap, _), b in zip(ws_, bs_):
                                ins = e.matmul(psum[b][:, 0:T], lhsT=wap[:, kc * 128:(kc + 1) * 128],
                                               rhs=xb[:, kc, 0:T], start=(kc == 0), stop=(kc == 7))
                            return ins
                        S.pe(fn, reads=[w[1] for w in ws_] + [("xb", kc)], writes=[PK(b) for b in bs_])
                    return bs_

                glu_banks = proj_multi([SB_IN + i_ for i_ in range(6)])
                for c in range(3):
                    bg = glu_banks[2 * c]
                    r = c % 2
                    S.act(lambda e, bg=bg, r=r, T=T: e.activation(out=sg[:, r, 0:T], in_=psum[bg][:, 0:T], func=AF.Sigmoid),
                          reads=[PK(bg)], writes=[("sg", r)])
                    bfree(bg)
                    ba = glu_banks[2 * c + 1]
                    if ti >= 1:
                        Tp = tiles[ti - 1][1]
                        S.pool(lambda e, l=l, c=c, Tp=Tp: e.tensor_copy(
                            out=abuf[:, l, c, 0:30], in_=abuf[:, l, c, Tp:Tp + 30]),
                            reads=[("abuf", l, c)], writes=[("abuf", l, c)])
                    S.dve(lambda e, ba=ba, r=r, l=l, c=c, T=T: e.tensor_tensor(
                        out=abuf[:, l, c, 30:30 + T], in0=psum[ba][:, 0:T], in1=sg[:, r, 0:T], op=ALU.mult),
                        reads=[PK(ba), ("sg", r)], writes=[("abuf", l, c)])
                    bfree(ba)
                    if ti == 0:
                        S.pool(lambda e, l=l, c=c: e.tensor_scalar(
                            out=abuf[:, l, c, 30:30 + H], in0=abuf[:, l, c, 30:30 + H],
                            scalar1=pcols[:, MASKC:MASKC + 1], scalar2=None, op0=ALU.mult),
                            reads=[("abuf", l, c), "pcols"], writes=[("abuf", l, c)])
                if l == 0 and pending[0] is not None:
                    pending[0]()
                    pending[0] = None
                lna_run = ln_begin(3, T, 3)
                for c in range(3):
                    b = balloc()
                    wl = []
                    for k in range(CK):
                        idx = c * CK + k
                        wap, wkey = W(pkey, l, SB_DIAG + idx // 8)
                        wl.append((wap[:, (idx % 8) * 128:(idx % 8 + 1) * 128], wkey))

                    def fn(e, b=b, c=c, wl=wl, l=l, T=T):
                        for k in range(CK):
                            ins = e.matmul(psum[b][:, 0:T], lhsT=wl[k][0], rhs=abuf[:, l, c, k:k + T],
                                           start=(k == 0), stop=(k == CK - 1))
                        return ins
                    S.pe(fn, reads=list(set(w[1] for w in wl)) + [("abuf", l, c)], writes=[PK(b)])
                    S.act(lambda e, b=b, c=c, l=l, T=T: e.activation(
                        out=yc[:, c, 0:T], in_=psum[b][:, 0:T], func=AF.Identity, bias=col(l, O_CAB, c), scale=1.0),
                        reads=[PK(b), "pcols"], writes=[("yc", c)])
                    bfree(b)
                    ln_feed(lna_run, yc[:, c, 0:T], ("yc", c))
                zc_ = [yc[:, c, 0:T] for c in range(3)]
                zk_ = [("yc", c) for c in range(3)]
                for tc in range(nt):
                    b = balloc()
                    waps = [W(pkey, l, SB_V + c) for c in range(3)]

                    def fn(e, tc=tc, b=b, waps=waps):
                        for c in range(3):
                            for kc in range(8):
                                ins = e.matmul(psum[b][:, c * 128:(c + 1) * 128],
                                               lhsT=xb[:, kc, tc * 128:(tc + 1) * 128],
                                               rhs=waps[c][0][:, kc * 128:(kc + 1) * 128],
                                               start=(kc == 0), stop=(kc == 7))
                        return ins
                    S.pe(fn, reads=[w[1] for w in waps] + xbk, writes=[PK(b)])
                    vflat = vg[:, tc].rearrange("p j r d -> p (j r d)")
                    S.act(lambda e, b=b, vflat=vflat: e.activation(out=vflat, in_=psum[b][:, 0:CW], func=AF.Gelu_apprx_tanh),
                          reads=[PK(b)], writes=[("vg", tc)])
                    bfree(b)
                    S.dve(lambda e, tc=tc, vflat=vflat: e.bn_stats(out=vst[:, tc, :], in_=vflat),
                          reads=[("vg", tc)], writes=[("vst", tc)])
                    S.dve(lambda e, tc=tc: e.bn_aggr(out=vmv[:, tc, :], in_=vst[:, tc, :]),
                          reads=[("vst", tc)], writes=["vmv"])

                for c in range(2):
                    b = proj(SB_Q + c)
                    S.act(lambda e, b=b, c=c, T=T: e.activation(out=qT[:, c, 0:T], in_=psum[b][:, 0:T], func=AF.Copy),
                          reads=[PK(b)], writes=[("qT", c)])
                    bfree(b)
                S.act(lambda e, nt=nt: e.activation(out=vrs[:, 0:nt], in_=vmv[:, 0:nt, 1], func=AF.Sqrt, bias=EPS, scale=1.0),
                      reads=["vmv"], writes=["vrs"])
                S.dve(lambda e, nt=nt: e.reciprocal(out=vrs[:, 0:nt], in_=vrs[:, 0:nt]), reads=["vrs"], writes=["vrs"])
                for tc in range(nt):
                    base = vpadb[:, tc, :, :, 0:64]
                    nap = [list(x) for x in base.ap]
                    nap[2][0] = 192
                    pview = bass.AP(base.tensor, base.offset, nap)
                    S.dve(lambda e, tc=tc, pview=pview: e.tensor_scalar(
                        out=pview, in0=vg[:, tc], scalar1=vmv[:, tc, 0:1], scalar2=vrs[:, tc:tc + 1],
                        op0=ALU.subtract, op1=ALU.mult), reads=[("vg", tc), "vmv", "vrs"], writes=[("vpadb", tc)])
                for h in range(4):
                    r = h % 2
                    for mc in range(2):
                        b = balloc()
                        S.pe(lambda e, b=b, h=h, r=r, mc=mc, l=l, T=T: e.matmul(
                            psum[b][:, 0:T], lhsT=kT[r * 64:(r + 1) * 64, l, h // 2, mc * 128:(mc + 1) * 128],
                            rhs=qT[r * 64:(r + 1) * 64, h // 2, 0:T], start=True, stop=True),
                            reads=["kT", ("qT", h // 2)], writes=[PK(b)])
                        S.act(lambda e, b=b, h=h, mc=mc, T=T: e.activation(
                            out=hid[:, h * 2 + mc, 0:T], in_=psum[b][:, 0:T], func=AF.Exp, scale=0.125),
                            reads=[PK(b)], writes=[("hid", h * 2 + mc)])
                        bfree(b)
                for c in range(3):
                    b = proj(SB_U + c)
                    S.act(lambda e, b=b, c=c, T=T: e.activation(out=ub[:, c, 0:T], in_=psum[b][:, 0:T], func=AF.Gelu_apprx_tanh),
                          reads=[PK(b)], writes=[("ub", c)])
                    bfree(b)
                for j in range(3):
                    b = balloc()

                    def fn(e, b=b, j=j, l=l, nt=nt):
                        for tc in range(nt):
                            e.matmul(psum[b][:, tc * 128:(tc + 1) * 128], lhsT=vpadb[:, tc, j, 0, :],
                                     rhs=wsT[:, l, 2 * j, :], start=True, stop=False)
                            ins = e.matmul(psum[b][:, tc * 128:(tc + 1) * 128], lhsT=vpadb[:, tc, j, 1, :],
                                           rhs=wsT[:, l, 2 * j + 1, :], start=False, stop=True)
                        return ins
                    S.pe(fn, reads=[("vpadb", tc) for tc in range(nt)] + ["wsT"], writes=[PK(b)])
                    for tc in range(nt):
                        S.dve(lambda e, b=b, j=j, l=l, tc=tc: e.scalar_tensor_tensor(
                            out=t1g[:, j % 2, tc * 128:(tc + 1) * 128], in0=psum[b][:, tc * 128:(tc + 1) * 128],
                            scalar=col(l, O_LVG, j), in1=bsb[:, l, j, :], op0=ALU.mult, op1=ALU.add),
                            reads=[PK(b), "bsb", "pcols"], writes=[("t1g", j % 2, tc)])
                    bfree(b)
                    S.pool(lambda e, j=j, T=T: e.tensor_tensor(
                        out=cat[:, 3 + j, 0:T], in0=t1g[:, j % 2, 0:T], in1=ub[:, j, 0:T], op=ALU.mult),
                        reads=[("t1g", j % 2, tc) for tc in range(nt)] + [("ub", j)], writes=[("cat", 3 + j)])
                for j in range(2):
                    bo = balloc()
                    bd = balloc()
                    pk = [("hid", (2 * j + r) * 2 + mc) for r in range(2) for mc in range(2)]

                    def fno(e, bo=bo, j=j, l=l, T=T):
                        i = 0
                        for r in range(2):
                            for mc in range(2):
                                ins = e.matmul(psum[bo][:, 0:T], lhsT=vpad[:, l, mc, 2 * j + r, :],
                                               rhs=hid[:, (2 * j + r) * 2 + mc, 0:T], start=(i == 0), stop=(i == 3))
                                i += 1
                        return ins

                    def fnd(e, bd=bd, j=j, T=T):
                        i = 0
                        for r in range(2):
                            for mc in range(2):
                                ins = e.matmul(psum[bd][:, 0:T], lhsT=eones[:, r, :],
                                               rhs=hid[:, (2 * j + r) * 2 + mc, 0:T], start=(i == 0), stop=(i == 3))
                                i += 1
                        return ins
                    S.pe(fno, reads=pk + ["vpad"], writes=[PK(bo)])
                    S.pe(fnd, reads=pk + ["eones"], writes=[PK(bd)])
                    S.dve(lambda e, bd=bd, j=j, T=T: e.reciprocal(out=rs[:, j, 0:T], in_=psum[bd][:, 0:T]),
                          reads=[PK(bd)], writes=[("rs", j)])
                    bfree(bd)
                    S.dve(lambda e, bo=bo, j=j, T=T: e.tensor_tensor(
                        out=cat[:, 6 + j, 0:T], in0=psum[bo][:, 0:T], in1=rs[:, j, 0:T], op=ALU.mult),
                        reads=[PK(bo), ("rs", j)], writes=[("cat", 6 + j)])
                    bfree(bo)
                lna_st = ln_flush(lna_run)
                ln1_run = ln_begin(8, T, 2)
                NG = 6
                wo = [W(pkey, l, SB_OUT + fo) for fo in range(NG)]
                wob = [balloc() for _ in range(NG)]
                for fo in range(NG):
                    def fn1(e, b=wob[fo], wap=wo[fo][0], T=T):
                        for i, kc in enumerate([3, 4, 5, 6, 7]):
                            ins = e.matmul(psum[b][:, 0:T], lhsT=wap[:, kc * 128:(kc + 1) * 128],
                                           rhs=cat[:, kc, 0:T], start=(i == 0), stop=False)
                        return ins
                    S.pe(fn1, reads=[wo[fo][1]] + [("cat", k) for k in range(3, 8)], writes=[PK(wob[fo])])
                ln_apply(lna_st, zc_, zk_, 3, CW, T,
                         [col(l, O_LAG, c) for c in range(3)], [col(l, O_LAB, c) for c in range(3)], AF.Silu,
                         [[(cat[:, c, 0:T], ("cat", c))] for c in range(3)])
                def wout_evac(b, fo, T=T):
                    S.dve(lambda e: e.scalar_tensor_tensor(
                        out=xf[:, fo, 0:T], in0=xf[:, fo, 0:T], scalar=ALPHA, in1=psum[b][:, 0:T],
                        op0=ALU.mult, op1=ALU.add), reads=[PK(b), ("xf", fo)], writes=[("xf", fo)])
                    bfree(b)
                    ln_feed(ln1_run, xf[:, fo, 0:T], ("xf", fo))
                for fo in range(NG):
                    def fn2(e, b=wob[fo], wap=wo[fo][0], T=T):
                        for i, kc in enumerate([0, 1, 2]):
                            ins = e.matmul(psum[b][:, 0:T], lhsT=wap[:, kc * 128:(kc + 1) * 128],
                                           rhs=cat[:, kc, 0:T], start=False, stop=(i == 2))
                        return ins
                    S.pe(fn2, reads=[wo[fo][1]] + [("cat", k) for k in range(3)], writes=[PK(wob[fo])])
                    wout_evac(wob[fo], fo)
                for fo in range(NG, 8):
                    wap, wkey = W(pkey, l, SB_OUT + fo)
                    b = balloc()

                    def fn(e, b=b, wap=wap, T=T):
                        for kc in range(8):
                            ins = e.matmul(psum[b][:, 0:T], lhsT=wap[:, kc * 128:(kc + 1) * 128],
                                           rhs=cat[:, kc, 0:T], start=(kc == 0), stop=(kc == 7))
                        return ins
                    S.pe(fn, reads=[wkey] + [("cat", k) for k in range(8)], writes=[PK(b)])
                    wout_evac(b, fo)
                ln_apply(ln_flush(ln1_run), [xf[:, k, 0:T] for k in range(8)], [("xf", k) for k in range(8)], 8, D, T,
                      [col(l, O_L1G, k) for k in range(8)], [col(l, O_L1B, k) for k in range(8)], AF.Identity,
                      [[(xb[:, k, 0:T], ("xb", k)), (xf[:, k, 0:T], ("xf", k))] for k in range(8)])
                ffn_first = proj_multi([SB_UP + i_ for i_ in range(6)])
                for j in range(NJ):
                    r = j % 2
                    if j < 3:
                        bg, bv = ffn_first[2 * j], ffn_first[2 * j + 1]
                    else:
                        bg = proj(SB_UP + 2 * j)
                        bv = proj(SB_UP + 2 * j + 1)
                    S.pool(lambda e, r=r, l=l, j=j: e.tensor_copy(out=gb[:, r, 0:2], in_=gst[:, l, j, :]),
                           reads=[("gst", l)], writes=[("gb", r)])
                    if ti == 0:
                        S.act(lambda e, bg=bg, r=r: e.activation(out=gb[:, r, 2:2 + H], in_=psum[bg][:, 0:H],
                                                                  func=AF.Identity, bias=0.0, scale=pcols[:, MASKC:MASKC + 1]),
                              reads=[PK(bg), "pcols"], writes=[("gb", r)])
                        S.act(lambda e, bg=bg, r=r, T=T: e.activation(out=gb[:, r, 2 + H:2 + T], in_=psum[bg][:, H:T], func=AF.Copy),
                              reads=[PK(bg)], writes=[("gb", r)])
                    else:
                        S.act(lambda e, bg=bg, r=r, T=T: e.activation(out=gb[:, r, 2:2 + T], in_=psum[bg][:, 0:T], func=AF.Copy),
                              reads=[PK(bg)], writes=[("gb", r)])
                    S.act(lambda e, bg=bg, r=r, T=T, l=l, j=j: e.activation(
                        out=ft1[:, r, 0:T], in_=psum[bg][:, 0:T], func=AF.Identity,
                        bias=col(l, O_CFB, j), scale=col(l, O_CFW, 2 * NJ + j)),
                        reads=[PK(bg), "pcols"], writes=[("ft1", r)])
                    bfree(bg)
                    S.pool(lambda e, r=r, l=l, j=j, T=T: e.tensor_copy(out=gst[:, l, j, :], in_=gb[:, r, T:T + 2]),
                           reads=[("gb", r)], writes=[("gst", l)])
                    S.dve(lambda e, r=r, l=l, j=j, T=T: e.scalar_tensor_tensor(
                        out=facc[:, r, 0:T], in0=gb[:, r, 1:1 + T], scalar=col(l, O_CFW, NJ + j),
                        in1=ft1[:, r, 0:T], op0=ALU.mult, op1=ALU.add),
                        reads=[("gb", r), ("ft1", r), "pcols"], writes=[("facc", r)])
                    S.dve(lambda e, r=r, l=l, j=j, T=T: e.scalar_tensor_tensor(
                        out=facc[:, r, 0:T], in0=gb[:, r, 0:T], scalar=col(l, O_CFW, j),
                        in1=facc[:, r, 0:T], op0=ALU.mult, op1=ALU.add),
                        reads=[("gb", r), ("facc", r), "pcols"], writes=[("facc", r)])
                    S.act(lambda e, r=r, T=T: e.activation(out=fs[:, r, 0:T], in_=facc[:, r, 0:T], func=AF.Silu),
                          reads=[("facc", r)], writes=[("fs", r)])
                    S.dve(lambda e, r=r, j=j, bv=bv, T=T: e.tensor_tensor(
                        out=hid[:, j, 0:T], in0=psum[bv][:, 0:T], in1=fs[:, r, 0:T], op=ALU.mult),
                        reads=[PK(bv), ("fs", r)], writes=[("hid", j)])
                    bfree(bv)
                if l == L - 1 and ti + 1 < len(tiles):
                    load_xb(ti + 1)
                if False:
                    ensure_produced(l, NUNIT - 1)
                    load_xf(ti + 1)
                else:
                    ln2_run = ln_begin(8, T, 1)
                    for hf in range(2):
                        for f4 in range(4):
                            fo = hf * 4 + f4
                            a_ = balloc()
                            for j in range(NJ):
                                wap, wkey = W(pkey, l, SB_DOWN + hf * 11 + j // 2)
                                wv = wap[:, (j % 2) * 512 + f4 * 128:(j % 2) * 512 + (f4 + 1) * 128]
                                S.pe(lambda e, wv=wv, j=j, a_=a_, T=T: e.matmul(
                                    psum[a_][:, 0:T], lhsT=wv, rhs=hid[:, j, 0:T], start=(j == 0), stop=(j == NJ - 1)),
                                    reads=[wkey, ("hid", j)], writes=[PK(a_)])
                            S.dve(lambda e, a_=a_, fo=fo, T=T: e.scalar_tensor_tensor(
                                out=xf[:, fo, 0:T], in0=xf[:, fo, 0:T], scalar=ALPHA, in1=psum[a_][:, 0:T],
                                op0=ALU.mult, op1=ALU.add), reads=[PK(a_), ("xf", fo)], writes=[("xf", fo)])
                            bfree(a_)
                            ln_feed(ln2_run, xf[:, fo, 0:T], ("xf", fo))
                    last = (l == L - 1)
                    outs = [([] if last else [(xb[:, k, 0:T], ("xb", k))]) + [(xf[:, k, 0:T], ("xf", k))] for k in range(8)]
                    zc2 = [xf[:, k, 0:T] for k in range(8)]
                    zk2 = [("xf", k) for k in range(8)]
                    st2 = ln_flush(ln2_run)

                    def fin(st2=st2, zc2=zc2, zk2=zk2, T=T, l=l, outs=outs, last=last, ti=ti, c0=c0):
                        ln_apply(st2, zc2, zk2, 8, D, T, [col(l, O_L2G, k) for k in range(8)],
                                 [col(l, O_L2B, k) for k in range(8)], AF.Identity, outs)
                        if last:
                            lo_ = H if ti == 0 else 0
                            dst = outT[:, c0 + lo_ - H:c0 - H + T].rearrange("(kc p) t -> p kc t", p=128)
                            out_ops.append(S.dma(lambda e, dst=dst, T=T, lo_=lo_: e.dma_start(out=dst, in_=xf[:, :, lo_:T]), "xst",
                                                 reads=[("xf", k) for k in range(8)], eng="act"))
                            if ti + 1 < len(tiles):
                                load_xf(ti + 1)
                    if last and ti + 1 < len(tiles):
                        pending[0] = fin
                    else:
                        fin()
        S.finish(final_waits=out_ops)
    return nc


def _pcols(L, params, mask):
    (conv_a_w, conv_a_b, ln_a_g, ln_a_b, ln1_g, ln1_b, ln2_g, ln2_b, conv_f_w, conv_f_b, ln_v_g, ln_v_b) = params
    pc = np.zeros((128, L * PL + 1), np.float32)
    for l in range(L):
        o = l * PL
        pc[:, o + O_CAW:o + O_CAW + 93] = conv_a_w[l].reshape(CK, 3, 128).transpose(2, 1, 0).reshape(128, 93)
        pc[:, o + O_CAB:o + O_CAB + 3] = conv_a_b[l].reshape(3, 128).T
        pc[:, o + O_LAG:o + O_LAG + 3] = ln_a_g[l].reshape(3, 128).T
        pc[:, o + O_LAB:o + O_LAB + 3] = ln_a_b[l].reshape(3, 128).T
        pc[:, o + O_L1G:o + O_L1G + 8] = ln1_g[l].reshape(8, 128).T
        pc[:, o + O_L1B:o + O_L1B + 8] = ln1_b[l].reshape(8, 128).T
        pc[:, o + O_L2G:o + O_L2G + 8] = ln2_g[l].reshape(8, 128).T
        pc[:, o + O_L2B:o + O_L2B + 8] = ln2_b[l].reshape(8, 128).T
        cfw = np.zeros((3, NJ * 128), np.float32)
        cfw[:, :DFF] = conv_f_w[l]
        pc[:, o + O_CFW:o + O_CFW + 66] = cfw.reshape(3, NJ, 128).transpose(2, 0, 1).reshape(128, 66)
        cfb = np.zeros((NJ * 128,), np.float32)
        cfb[:DFF] = conv_f_b[l]
        pc[:, o + O_CFB:o + O_CFB + NJ] = cfb.reshape(NJ, 128).T
        pc[:, o + O_LVG:o + O_LVG + 3] = ln_v_g[l].reshape(3, 128).T
        pc[:, o + O_LVB:o + O_LVB + 3] = ln_v_b[l].reshape(3, 128).T
    pc[:, L * PL] = mask
    return pc


_NC_CACHE = {}


def _get_nc(L, H, n_main):
    key = (L, H, n_main)
    if key not in _NC_CACHE:
        _NC_CACHE[key] = build(L, H, n_main)
    return _NC_CACHE[key]


def _run(x, mem, layers, P, H, n_main=8):
    L = len(layers)
    sl = lambda a: np.ascontiguousarray(a[layers])
    nc = _get_nc(L, H, n_main)
    ident = np.eye(128, dtype=np.float32)
    triu = np.triu(np.ones((128, 128), np.float32))
    bs = sl(P["b_s"])
    bsb = np.ascontiguousarray(
        np.repeat(bs.reshape(L, 3, 2, 1, 128), 64, axis=3).transpose(2, 3, 0, 1, 4).reshape(128, L * 3 * 128))
    wsT = np.ascontiguousarray(sl(P["w_s"]).transpose(0, 1, 3, 2))
    params = tuple(sl(P[k]) for k in ("conv_a_w", "conv_a_b", "ln_a_g", "ln_a_b", "ln1_g", "ln1_b",
                                      "ln2_g", "ln2_b", "conv_f_w", "conv_f_b", "ln_v_g", "ln_v_b"))
    shared = dict(ident=ident, triu=triu, bsb=bsb, wsT=wsT,
                  w_in=sl(P["w_in"]), w_out=sl(P["w_out"]), w_up=sl(P["w_up"]), w_down=sl(P["w_down"]),
                  w_mk=sl(P["w_mk"]), w_mv=sl(P["w_mv"]))
    ntok = n_main * TW
    in_maps = []
    for c in range(NCORES):
        b = c // 4
        t0 = (c % 4) * TOK_PER_CORE
        xt = np.zeros((D, H + ntok), np.float32)
        lo = t0 - H
        if lo >= 0:
            xt[:, :] = x[b, lo:t0 + ntok, :].T
        else:
            xt[:, H:] = x[b, t0:t0 + ntok, :].T
        m = dict(shared)
        m["xT"] = xt
        m["memT"] = np.ascontiguousarray(mem[b].T)
        m["pcols"] = _pcols(L, params, 0.0 if t0 == 0 else 1.0)
        in_maps.append(m)
    res = run_bass_kernel_spmd(nc, in_maps, core_ids=list(range(NCORES)))
    out = np.empty((BATCH, SEQ, D), np.float32)
    for c in range(NCORES):
        b = c // 4
        t0 = (c % 4) * TOK_PER_CORE
        out[b, t0:t0 + ntok, :] = res.results[c]["outT"].T
    return out


FUSED = True


def kernel(x, mem, **P):
    x = np.asarray(x, np.float32)
    mem = np.asarray(mem, np.float32)
    P = {k: np.asarray(v, np.float32) for k, v in P.items()}
    if FUSED:
        return _run(x, mem, [0, 1], P, H=256)
    for l in range(DEPTH):
        x = _run(x, mem, [l], P, H=128)
    return x
```

```python
import contextlib
import numpy as np
import concourse.bass as bass
import concourse.mybir as mybir
from concourse.bass_utils import run_bass_kernel_spmd

F32 = mybir.dt.float32
BF16 = mybir.dt.bfloat16
AF = mybir.ActivationFunctionType
ALU = mybir.AluOpType

D = 1024
SEQ = 16384
BATCH = 2
DEPTH = 2
NCORES = 8
TOK_PER_CORE = SEQ * BATCH // NCORES
CW = 384
XW = 256
INW = 1792
CK = 31
DFF = 2752
NJ = 22
NMEM = 256
ALPHA = (2.0 * DEPTH) ** 0.25
EPS = 1e-5
TW = 512
NSLOT = 5
PL = 93 + 9 + 32 + 66 + 22 + 6
O_CAW, O_CAB, O_LAG, O_LAB = 0, 93, 96, 99
O_L1G, O_L1B, O_L2G, O_L2B = 102, 110, 118, 126
O_CFW, O_CFB = 134, 200
O_LVG, O_LVB = 222, 225

SB_IN = 0
SB_DIAG = 6
SB_V = 18
SB_Q = 21
SB_U = 23
SB_OUT = 26
SB_UP = 34
SB_DOWN = 78
NSB = 100
NUNIT = 25

COMPUTE = ("pe", "act", "dve", "pool")
QUEUES = ("sp",)
SAME_ENG_DIST = 1 << 30


class Op:
    __slots__ = ("eng", "fn", "reads", "writes", "seq", "deps", "inc", "val",
                 "dma", "clock", "waits", "name", "cid", "ndma")

    def __init__(self, eng, fn, reads, writes, dma, name):
        self.eng = eng
        self.fn = fn
        self.reads = reads
        self.writes = writes
        self.dma = dma
        self.name = name
        self.deps = []
        self.inc = False
        self.val = 0
        self.waits = []
        self.clock = None
        self.cid = None
        self.ndma = 1


class Sched:
    def __init__(self, nc, stack):
        self.nc = nc
        self.stack = stack
        self.ops = []
        self.eng_ops = {e: [] for e in COMPUTE + QUEUES}
        self.sems = {}

    def op(self, eng, fn, reads=(), writes=(), dma=None, name=""):
        o = Op(eng, fn, tuple(reads), tuple(writes), dma, name)
        o.seq = len(self.eng_ops[eng]) + 1
        self.eng_ops[eng].append(o)
        self.ops.append(o)
        return o

    def pe(self, fn, reads=(), writes=(), name=""):
        return self.op("pe", fn, reads, writes, None, name)

    def act(self, fn, reads=(), writes=(), name=""):
        return self.op("act", fn, reads, writes, None, name)

    def dve(self, fn, reads=(), writes=(), name=""):
        return self.op("dve", fn, reads, writes, None, name)

    def pool(self, fn, reads=(), writes=(), name=""):
        return self.op("pool", fn, reads, writes, None, name)

    def dma(self, fn, sem, reads=(), writes=(), eng="sp", name="", n=1):
        o = self.op(eng, fn, reads, writes, ("dma", sem), name)
        o.ndma = n
        return o

    def _analyze(self):
        last_w = {}
        readers = {}
        last_dma_on_sem = {}
        for o in self.ops:
            deps = {}
            for b in o.reads:
                w = last_w.get(b)
                if w is not None:
                    deps[id(w)] = w
            for b in o.writes:
                w = last_w.get(b)
                if w is not None:
                    deps[id(w)] = w
                for r in readers.get(b, ()):
                    deps[id(r)] = r
            if o.dma is not None:
                p = last_dma_on_sem.get(o.dma)
                if p is not None:
                    deps[id(p)] = p
                last_dma_on_sem[o.dma] = o
            deps.pop(id(o), None)
            o.deps = list(deps.values())
            for b in o.reads:
                readers.setdefault(b, []).append(o)
            for b in o.writes:
                last_w[b] = o
                readers[b] = []
        for o in self.ops:
            o.cid = o.dma if o.dma is not None else o.eng
        eng_clock = {e: {} for e in self.eng_ops}
        ordidx = {}
        per_cid_seq = {}
        for o in self.ops:
            n = per_cid_seq.get(o.cid, 0) + 1
            per_cid_seq[o.cid] = n
            ordidx[id(o)] = n
        for o in self.ops:
            clk = eng_clock[o.eng]
            best = {}
            for d in o.deps:
                dn = ordidx[id(d)]
                if d.dma is None and d.eng == o.eng:
                    if o.eng == "pe":
                        continue
                    if o.dma is None and o.seq - d.seq > SAME_ENG_DIST:
                        continue
                if clk.get(d.cid, 0) >= dn:
                    continue
                if d.cid not in best or dn > ordidx[id(best[d.cid])]:
                    best[d.cid] = d
            final = []
            for d in sorted(best.values(), key=lambda d: -len(d.clock)):
                if clk.get(d.cid, 0) >= ordidx[id(d)]:
                    continue
                final.append(d)
                for k, v in d.clock.items():
                    if clk.get(k, 0) < v:
                        clk[k] = v
                clk[d.cid] = max(clk.get(d.cid, 0), ordidx[id(d)])
            for d in final:
                d.inc = True
            o.waits = final
            o.clock = dict(clk)
        cnt = {}
        for o in self.ops:
            if o.dma is not None:
                cnt[o.cid] = cnt.get(o.cid, 0) + 16 * o.ndma
                o.val = cnt[o.cid]
                o.inc = True
            elif o.inc:
                cnt[o.cid] = cnt.get(o.cid, 0) + 1
                o.val = cnt[o.cid]

    def finish(self, final_waits=()):
        nc = self.nc
        self._analyze()
        cids = set(o.cid for o in self.ops if o.inc)
        for cid in sorted(cids, key=str):
            nm = "s_" + (cid if isinstance(cid, str) else "d_" + str(cid[1]))
            self.sems[cid] = self.stack.enter_context(nc.semaphore(nm))
        block = self.stack.enter_context(nc.Block())
        sems = self.sems

        def emit(engobj, ename):
            for o in self.eng_ops[ename]:
                for d in o.waits:
                    engobj.wait_ge(sems[d.cid], d.val)
                ins = o.fn(engobj)
                if o.dma is not None:
                    if not isinstance(ins, (list, tuple)):
                        ins = [ins]
                    assert len(ins) == o.ndma, o.name
                    for i_ in ins:
                        i_.then_inc(sems[o.cid], 16)
                elif o.inc:
                    assert ins is not None, o.name
                    ins.then_inc(sems[o.cid], 1)
            if ename == "sp":
                for d in final_waits:
                    engobj.wait_ge(sems[d.cid], d.val)

        @block.tensor
        def _(e):
            emit(e, "pe")

        @block.scalar
        def _(e):
            emit(e, "act")

        @block.vector
        def _(e):
            emit(e, "dve")

        @block.gpsimd
        def _(e):
            emit(e, "pool")

        @block.sync
        def _(e):
            emit(e, "sp")


def build(L, H, n_main=8, last_out=True):
    NT = H + n_main * TW
    NCOL = L * PL + 1
    MASKC = L * PL
    nc = bass.Bass("TRN2", target_bir_lowering=False)

    def din(name, shape, dt=F32):
        return nc.dram_tensor(name, list(shape), dt, kind="ExternalInput").ap()

    xT = din("xT", [D, NT])
    memT = din("memT", [D, NMEM])
    pcols_d = din("pcols", [128, NCOL])
    ident_d = din("ident", [128, 128])
    triu_d = din("triu", [128, 128])
    bsb_d = din("bsb", [128, L * 3 * 128])
    wsT_d = din("wsT", [L, 6, 128, 128])
    w_in = din("w_in", [L, D, INW])
    w_out = din("w_out", [L, D, D])
    w_up = din("w_up", [L, D, 2 * DFF])
    w_down = din("w_down", [L, DFF, D])
    w_mk = din("w_mk", [L, D, XW])
    w_mv = din("w_mv", [L, D, XW])
    outT = nc.dram_tensor("outT", [D, n_main * TW], F32, kind="ExternalOutput").ap()
    wst = nc.dram_tensor("wst", [L, NUNIT, 128, 4096], BF16, kind="Internal").ap()

    st = contextlib.ExitStack()
    with st:
        S = Sched(nc, st)

        def sb(name, shape, dt):
            return st.enter_context(nc.sbuf_tensor(name, list(shape), dt))

        pcols = sb("pcols_s", [128, NCOL], F32)
        ident = sb("ident_s", [128, 128], F32)
        ones_bf = sb("ones_bf", [128, 128], BF16)
        eones = sb("eones", [128, 2, 128], BF16)
        bsb = sb("bsb_s", [128, L, 3, 128], F32)
        wsT = sb("wsT_s", [128, L, 6, 128], BF16)
        kT = sb("kT", [128, L, 2, NMEM], BF16)
        vpad = sb("vpad", [128, L, 2, 4, 128], BF16)
        abuf = sb("abuf", [128, L, 3, 30 + TW], BF16)
        gst = sb("gst", [128, L, NJ, 2], F32)
        ring = sb("ring", [128, NSLOT, 4096], BF16)
        xf = sb("xf", [128, 8, TW], F32)
        xb = sb("xb", [128, 8, TW], BF16)
        sg = sb("sg", [128, 2, TW], F32)
        ub = sb("ub", [128, 3, TW], F32)
        vg = sb("vg", [128, 4, 3, 2, 64], F32)
        vst = sb("vst", [128, 4, 6], F32)
        vmv = sb("vmv", [128, 4, 2], F32)
        vrs = sb("vrs", [128, 4], F32)
        vpadb = sb("vpadb", [128, 4, 3, 2, 128], BF16)
        qT = sb("qT", [128, 2, TW], BF16)
        rs = sb("rs", [128, 2, TW], F32)
        cat = sb("cat", [128, 8, TW], BF16)
        yc = sb("yc", [128, 3, TW], F32)
        zb = sb("zb", [128, 4, TW], BF16)
        zq = sb("zq", [128, 4, TW], BF16)
        lmsq = sb("lmsq", [128, TW], F32)
        lrstd = sb("lrstd", [128, TW], F32)
        lnmr = sb("lnmr", [128, TW], F32)
        lt = sb("lt", [128, 2, TW], F32)
        ft1 = sb("ft1", [128, 2, TW], F32)
        gb = sb("gb", [128, 2, TW + 2], F32)
        facc = sb("facc", [128, 2, TW], F32)
        fs = sb("fs", [128, 2, TW], F32)
        hid = sb("hid", [128, NJ, TW], BF16)
        t1g = sb("t1g", [128, 2, TW], F32)
        scr = sb("scr", [128, 8], F32)

        psum = [st.enter_context(nc.psum_tensor("ps%d" % i, [128, TW], F32)) for i in range(8)]
        free_banks = list(range(8))

        def balloc():
            return free_banks.pop(0)

        def bfree(b):
            free_banks.append(b)

        def PK(b):
            return ("ps", b)

        def col(l, off, i=0):
            c = l * PL + off + i
            return pcols[:, c:c + 1]

        def ld(dst, src, sem, key, eng="sp"):
            return S.dma(lambda e: e.dma_start(out=dst, in_=src), sem, writes=[key], eng=eng)

        ld(pcols[:], pcols_d, "c0", "pcols")
        ld(ident[:], ident_d, "c1", "ident")
        ld(bsb[:].rearrange("p l j t -> p (l j t)"), bsb_d, "c4", "bsb")
        S.pool(lambda e: e.memset(ones_bf[:], 1.0), writes=["ones"])
        S.pool(lambda e: e.memset(eones[:], 0.0), writes=["eones"])
        S.pool(lambda e: e.memset(eones[:, 0, 0:64], 1.0), reads=[], writes=["eones"])
        S.pool(lambda e: e.memset(eones[:, 1, 64:128], 1.0), reads=[], writes=["eones"])
        S.pool(lambda e: e.memset(vpadb[:].rearrange("p a b c d -> p (a b c d)"), 0.0), writes=["vpadb"])
        S.pool(lambda e: e.memset(vpad[:].rearrange("p a b c d -> p (a b c d)"), 0.0), writes=["vpad"])
        S.pool(lambda e: e.memset(abuf[:].rearrange("p a b c -> p (a b c)"), 0.0),
               writes=[("abuf", l, c) for l in range(L) for c in range(3)])
        S.pool(lambda e: e.memset(gst[:].rearrange("p a b c -> p (a b c)"), 0.0),
               writes=[("gst", l) for l in range(L)])

        NST = 4

        def st32(i):
            return xf[:, 2 * i:2 * i + 2, :].rearrange("p a b -> p (a b)")

        def st32k(i):
            return [("xf", 2 * i), ("xf", 2 * i + 1)]

        def st16(i):
            return hid[:, 2 * i:2 * i + 2, :].rearrange("p a b -> p (a b)")

        def st16k(i):
            return [("hid", 2 * i), ("hid", 2 * i + 1)]

        cast_rr = [0]

        def cast_op(dst, src, reads, writes):
            k = cast_rr[0] % 3
            cast_rr[0] += 1
            if k == 0:
                S.act(lambda e: e.activation(out=dst, in_=src, func=AF.Copy), reads=reads, writes=writes)
            elif k == 1:
                S.dve(lambda e: e.tensor_copy(out=dst, in_=src), reads=reads, writes=writes)
            else:
                S.pool(lambda e: e.tensor_copy(out=dst, in_=src), reads=reads, writes=writes)

        ld(st32(0)[:, 0:128], triu_d, "c5", ("xf", 0))
        for l in range(L):
            for h2 in range(2):
                src = wsT_d[l, 3 * h2:3 * h2 + 3].rearrange("h s t -> s h t")
                dst = st32(1)[:, 0:384].rearrange("p (h t) -> p h t", h=3)
                S.dma(lambda e, dst=dst, src=src: e.dma_start(out=dst, in_=src), "pin1",
                      writes=st32k(1))
                for hh in range(3):
                    h = 3 * h2 + hh
                    S.dve(lambda e, l=l, h=h, hh=hh: e.tensor_tensor(
                        out=wsT[:, l, h, :], in0=st32(1)[:, hh * 128:(hh + 1) * 128],
                        in1=st32(0)[:, 0:128], op=ALU.mult),
                        reads=st32k(1) + [("xf", 0)], writes=["wsT"])

        memTb = hid[:, 8:12, :].rearrange("p a b -> p (a b)").rearrange("p (k m) -> p k m", k=8)
        memk = [("hid", j) for j in range(8, 12)]
        wkvb = hid[:, 12:16, :].rearrange("p a b -> p (a b)").rearrange("p (k m) -> p k m", k=8)
        wkvk = [("hid", j) for j in range(12, 16)]

        def load_kv(dst, dkeys, src2d):
            for half in range(2):
                slot = 2 + half
                src = src2d[half * 512:(half + 1) * 512, :].rearrange("(k p) m -> p k m", p=128)
                dstage = st32(slot).rearrange("p (k m) -> p k m", k=4)
                S.dma(lambda e, d=dstage, s=src: e.dma_start(out=d, in_=s), "pin%d" % slot,
                      writes=st32k(slot))
                cast_op(dst[:, half * 4:(half + 1) * 4, :], dstage, st32k(slot), dkeys)

        load_kv(memTb, memk, memT)
        for l in range(L):
            load_kv(wkvb, wkvk, w_mk[l])
            for c2 in range(2):
                b = balloc()

                def fn(e, l=l, c2=c2, b=b):
                    for kc in range(8):
                        ins = e.matmul(psum[b][:, 0:NMEM], lhsT=wkvb[:, kc, c2 * 128:(c2 + 1) * 128],
                                       rhs=memTb[:, kc, :], start=(kc == 0), stop=(kc == 7))
                    return ins
                S.pe(fn, reads=memk + wkvk, writes=[PK(b)])
                S.act(lambda e, l=l, c2=c2, b=b: e.activation(out=kT[:, l, c2, :], in_=psum[b][:, 0:NMEM], func=AF.Copy),
                      reads=[PK(b)], writes=["kT"])
                bfree(b)
            load_kv(wkvb, wkvk, w_mv[l])
            for mc in range(2):
                b = balloc()

                def fn(e, l=l, mc=mc, b=b):
                    for kc in range(8):
                        ins = e.matmul(psum[b][:, 0:XW], lhsT=memTb[:, kc, mc * 128:(mc + 1) * 128],
                                       rhs=wkvb[:, kc, :], start=(kc == 0), stop=(kc == 7))
                    return ins
                S.pe(fn, reads=memk + wkvk, writes=[PK(b)])
                for h in range(4):
                    r = h % 2
                    S.dve(lambda e, l=l, mc=mc, h=h, r=r, b=b: e.tensor_copy(
                        out=vpad[:, l, mc, h, r * 64:(r + 1) * 64], in_=psum[b][:, h * 64:(h + 1) * 64]),
                        reads=[PK(b)], writes=["vpad"])
                bfree(b)

        for l in range(L):
            for j in range(3):
                b = balloc()

                def fn(e, l=l, j=j, b=b):
                    e.matmul(psum[b][:, 0:128], lhsT=eones[:, 0, :], rhs=wsT[:, l, 2 * j, :], start=True, stop=False)
                    return e.matmul(psum[b][:, 0:128], lhsT=eones[:, 1, :], rhs=wsT[:, l, 2 * j + 1, :],
                                    start=False, stop=True)
                S.pe(fn, reads=["eones", "wsT"], writes=[PK(b)])
                S.dve(lambda e, l=l, j=j, b=b: e.scalar_tensor_tensor(
                    out=bsb[:, l, j, :], in0=psum[b][:, 0:128], scalar=col(l, O_LVB, j), in1=bsb[:, l, j, :],
                    op0=ALU.mult, op1=ALU.add), reads=[PK(b), "bsb", "pcols"], writes=["bsb"])
                bfree(b)

        stg = sb("stg", [128, 4, 1024], F32)
        ring_state = {"count": 0, "loaded": {}, "produced": [], "stg": 0, "rr": 0, "res": {}}
        in_cols = {0: CW, 1: 0, 2: CW + 128, 3: 128, 4: CW + 256, 5: 256}
        for i_ in range(3):
            in_cols[SB_V + i_] = 1152 + 128 * i_
            in_cols[SB_U + i_] = 768 + 128 * i_
        for i_ in range(2):
            in_cols[SB_Q + i_] = 1536 + 128 * i_

        def kc_view(src2d, c0, cw):
            return src2d.rearrange("(kc p) n -> p kc n", p=128)[:, :, c0:c0 + cw]

        def sub_source(l, s):
            kv = lambda d: d.rearrange("p (k c) -> p k c", k=8)
            if s in in_cols:
                return [(kv, kc_view(w_in[l], in_cols[s], 128))], False
            if SB_OUT <= s < SB_UP:
                return [(kv, kc_view(w_out[l], (s - SB_OUT) * 128, 128))], False
            if SB_UP <= s < SB_DOWN:
                j, gv = (s - SB_UP) // 2, (s - SB_UP) % 2
                cw = 128 if j < NJ - 1 else 64
                return [(lambda d, cw=cw: kv(d)[:, :, 0:cw], kc_view(w_up[l], gv * DFF + j * 128, cw))], cw < 128
            hf, jj = (s - SB_DOWN) // 11, (s - SB_DOWN) % 11
            if jj < 10:
                src = w_down[l][jj * 256:(jj + 1) * 256, hf * 512:(hf + 1) * 512].rearrange("(jl p) c -> p jl c", p=128)
                return [(lambda d: d.rearrange("p (a c) -> p a c", a=2), src)], False
            s0 = w_down[l][2560:2688, hf * 512:(hf + 1) * 512]
            s1 = w_down[l][2688:2752, hf * 512:(hf + 1) * 512]
            return [(lambda d: d[:, 0:512], s0), (lambda d: d[0:64, 512:1024], s1)], True

        def cast2(dst, src, reads, writes):
            k = ring_state["rr"] % 2
            ring_state["rr"] += 1
            if k == 0:
                S.dve(lambda e: e.tensor_copy(out=dst, in_=src), reads=reads, writes=writes)
            else:
                S.act(lambda e: e.activation(out=dst, in_=src, func=AF.Copy), reads=reads, writes=writes)

        def produce_unit(l, unit):
            slot = ring_state["count"] % NSLOT
            ring_state["count"] += 1
            for q in range(4):
                s = 4 * unit + q
                dst16 = ring[:, slot, q * 1024:(q + 1) * 1024]
                rk = ("ring", slot, q)
                if SB_DIAG <= s < SB_V:
                    first = (s - SB_DIAG) * 8
                    on_act = (s % 2 == 1)

                    def fn(e, dst16=dst16, first=first, l=l, on_act=on_act):
                        for qq in range(8):
                            idx = first + qq
                            o = dst16[:, qq * 128:(qq + 1) * 128]
                            if idx >= 93:
                                ins = e.memset(o, 0.0) if not on_act else e.activation(
                                    out=o, in_=ident[:], func=AF.Copy, scale=0.0)
                            elif on_act:
                                ins = e.activation(out=o, in_=ident[:], func=AF.Identity, bias=0.0,
                                                   scale=col(l, O_CAW, idx))
                            else:
                                ins = e.tensor_scalar(out=o, in0=ident[:], scalar1=col(l, O_CAW, idx),
                                                      scalar2=None, op0=ALU.mult)
                        return ins
                    (S.act if on_act else S.dve)(fn, reads=["ident", "pcols"], writes=[rk])
                else:
                    k = ring_state["stg"] % 4
                    ring_state["stg"] += 1
                    d32 = stg[:, k, :]
                    sk = ("stg", k)
                    parts, zero_first = sub_source(l, s)
                    if zero_first:
                        S.pool(lambda e, d32=d32: e.memset(d32, 0.0), writes=[sk])
                    S.dma(lambda e, parts=parts, d32=d32: [e.dma_start(out=vf(d32), in_=src) for (vf, src) in parts],
                          "pin%d" % k, writes=[sk], n=len(parts))
                    cast2(dst16, d32, [sk], [rk])
            S.dma(lambda e, slot=slot, l=l, unit=unit: e.dma_start(out=wst[l, unit], in_=ring[:, slot, :]),
                  "pout%d" % (slot % 2), reads=[("ring", slot, q) for q in range(4)],
                  writes=[("wst", l, unit)], eng="pool")
            ring_state["loaded"][(0, l, unit)] = slot
            ring_state["res"][slot] = (0, l, unit)

        def ensure_produced(l, unit):
            order = [(ll, uu) for ll in range(L) for uu in range(NUNIT)]
            tgt = order.index((l, unit))
            while len(ring_state["produced"]) <= tgt:
                ll, uu = order[len(ring_state["produced"])]
                produce_unit(ll, uu)
                ring_state["produced"].append((ll, uu))

        def W(pkey, l, s):
            ti_ = pkey[0]
            unit = s // 4
            key = (ti_, l, unit)
            if ti_ == 0:
                order_idx = l * NUNIT + unit
                nxt = min(order_idx + 1, L * NUNIT - 1)
                ensure_produced(nxt // NUNIT, nxt % NUNIT)
            elif key not in ring_state["loaded"]:
                slot = ring_state["count"] % NSLOT
                ring_state["count"] += 1
                S.dma(lambda e, slot=slot, l=l, unit=unit: e.dma_start(out=ring[:, slot, :], in_=wst[l, unit]),
                      "ring%d" % slot, reads=[("wst", l, unit)],
                      writes=[("ring", slot, q) for q in range(4)])
                ring_state["loaded"][key] = slot
                ring_state["res"][slot] = key
            slot = ring_state["loaded"][key]
            assert ring_state["res"][slot] == key, ("ring slot evicted before use", key, ring_state["res"][slot])
            o = (s % 4) * 1024
            return ring[:, slot, o:o + 1024], ("ring", slot, s % 4)

        def ln_begin(n, T, lag):
            return {"b": None, "n": n, "T": T, "lag": lag, "q": [], "fed": 0}

        def _ln_mm(st):
            c, r = st["q"].pop(0)
            if st["b"] is None:
                st["b"] = (balloc(), balloc())
            b1, b2 = st["b"]
            n, T = st["n"], st["T"]
            S.pe(lambda e: e.matmul(psum[b1][:, 0:T], lhsT=ones_bf[:], rhs=zb[:, r, 0:T],
                                    start=(c == 0), stop=(c == n - 1)),
                 reads=["ones", ("zb", r)], writes=[PK(b1)])
            S.pe(lambda e: e.matmul(psum[b2][:, 0:T], lhsT=ones_bf[:], rhs=zq[:, r, 0:T],
                                    start=(c == 0), stop=(c == n - 1)),
                 reads=["ones", ("zq", r)], writes=[PK(b2)])

        def ln_feed(st, zc, zkey):
            c = st["fed"]
            st["fed"] += 1
            r = c % 4
            T = st["T"]
            S.act(lambda e: e.activation(out=zb[:, r, 0:T], in_=zc, func=AF.Copy), reads=[zkey], writes=[("zb", r)])
            S.act(lambda e: e.activation(out=zq[:, r, 0:T], in_=zc, func=AF.Square), reads=[zkey], writes=[("zq", r)])
            st["q"].append((c, r))
            while len(st["q"]) > st["lag"]:
                _ln_mm(st)

        def ln_flush(st):
            while st["q"]:
                _ln_mm(st)
            return st["b"]

        def ln_stats(z, zkeys, n, T):
            st = ln_begin(n, T, 0)
            for c in range(n):
                ln_feed(st, z[c], zkeys[c])
            return ln_flush(st)

        def ln_apply(stt, z, zkeys, n, Dn, T, gcols, bcols, func, outs):
            b1, b2 = stt
            inv = 1.0 / Dn
            S.act(lambda e: e.activation(out=lmsq[:, 0:T], in_=psum[b1][:, 0:T], func=AF.Square, scale=inv),
                  reads=[PK(b1)], writes=["lmsq"])
            S.dve(lambda e: e.scalar_tensor_tensor(out=lrstd[:, 0:T], in0=psum[b2][:, 0:T], scalar=inv,
                                                   in1=lmsq[:, 0:T], op0=ALU.mult, op1=ALU.subtract),
                  reads=[PK(b2), "lmsq"], writes=["lrstd"])
            S.act(lambda e: e.activation(out=lrstd[:, 0:T], in_=lrstd[:, 0:T], func=AF.Sqrt, bias=EPS, scale=1.0),
                  reads=["lrstd"], writes=["lrstd"])
            S.dve(lambda e: e.reciprocal(out=lrstd[:, 0:T], in_=lrstd[:, 0:T]), reads=["lrstd"], writes=["lrstd"])
            S.dve(lambda e: e.scalar_tensor_tensor(out=lnmr[:, 0:T], in0=psum[b1][:, 0:T], scalar=-inv,
                                                   in1=lrstd[:, 0:T], op0=ALU.mult, op1=ALU.mult),
                  reads=[PK(b1), "lrstd"], writes=["lnmr"])
            bfree(b1)
            bfree(b2)
            for c in range(n):
                r = c % 2
                S.dve(lambda e, c=c, r=r: e.tensor_tensor(out=lt[:, r, 0:T], in0=z[c], in1=lrstd[:, 0:T], op=ALU.mult),
                      reads=[zkeys[c], "lrstd"], writes=[("lt", r)])
                (S.pool if c % 2 == 1 else S.dve)(
                    lambda e, r=r: e.tensor_tensor(out=lt[:, r, 0:T], in0=lt[:, r, 0:T], in1=lnmr[:, 0:T], op=ALU.add),
                    reads=[("lt", r), "lnmr"], writes=[("lt", r)])
                for (oap, okey) in outs[c]:
                    S.act(lambda e, c=c, r=r, oap=oap: e.activation(out=oap, in_=lt[:, r, 0:T], func=func,
                                                                      bias=bcols[c], scale=gcols[c]),
                          reads=[("lt", r), "pcols"], writes=[okey])

        def ln_fm(z, zkeys, n, Dn, T, gcols, bcols, func, outs):
            stt = ln_stats(z, zkeys, n, T)
            ln_apply(stt, z, zkeys, n, Dn, T, gcols, bcols, func, outs)

        tiles = [(0, H)] + [(H + i * TW, TW) for i in range(n_main)]
        out_ops = []
        pending = [None]

        def load_xb(ti_):
            c0_, T_ = tiles[ti_]
            src = xT[:, c0_:c0_ + T_].rearrange("(kc p) t -> p kc t", p=128)
            S.dma(lambda e: e.dma_start(out=xb[:, :, 0:T_], in_=src), "xlb",
                  writes=[("xb", k) for k in range(8)], eng="pool")

        def load_xf(ti_):
            c0_, T_ = tiles[ti_]
            src = xT[:, c0_:c0_ + T_].rearrange("(kc p) t -> p kc t", p=128)
            S.dma(lambda e: e.dma_start(out=xf[:, :, 0:T_], in_=src), "xld",
                  writes=[("xf", k) for k in range(8)], eng="act")
        for ti, (c0, T) in enumerate(tiles):
            nt = T // 128
            if ti == 0:
                load_xb(0)
                load_xf(0)
            for l in range(L):
                pkey = (ti, l)
                xbk = [("xb", k) for k in range(8)]

                def proj(s, T=T, pkey=pkey, l=l, xbk=xbk):
                    wap, wkey = W(pkey, l, s)
                    b = balloc()

                    def fn(e):
                        for kc in range(8):
                            ins = e.matmul(psum[b][:, 0:T], lhsT=wap[:, kc * 128:(kc + 1) * 128],
                                           rhs=xb[:, kc, 0:T], start=(kc == 0), stop=(kc == 7))
                        return ins
                    S.pe(fn, reads=[wkey] + xbk, writes=[PK(b)])
                    return b

                def proj_multi(subs, T=T, pkey=pkey, l=l):
                    ws_ = [W(pkey, l, s_) for s_ in subs]
                    bs_ = [balloc() for _ in subs]
                    for kc in range(8):
                        def fn(e, kc=kc):
                            for (wap, _), b in zip(ws_, bs_):
                                ins = e.matmul(psum[b][:, 0:T], lhsT=wap[:, kc * 128:(kc + 1) * 128],
                                               rhs=xb[:, kc, 0:T], start=(kc == 0), stop=(kc == 7))
                            return ins
                        S.pe(fn, reads=[w[1] for w in ws_] + [("xb", kc)], writes=[PK(b) for b in bs_])
                    return bs_

                glu_banks = proj_multi([SB_IN + i_ for i_ in range(6)])
                for c in range(3):
                    bg = glu_banks[2 * c]
                    r = c % 2
                    S.act(lambda e, bg=bg, r=r, T=T: e.activation(out=sg[:, r, 0:T], in_=psum[bg][:, 0:T], func=AF.Sigmoid),
                          reads=[PK(bg)], writes=[("sg", r)])
                    bfree(bg)
                    ba = glu_banks[2 * c + 1]
                    if ti >= 1:
                        Tp = tiles[ti - 1][1]
                        if ti == 1:
                            S.pool(lambda e, l=l, c=c, Tp=Tp: e.tensor_scalar(
                                out=abuf[:, l, c, 0:30], in0=abuf[:, l, c, Tp:Tp + 30],
                                scalar1=pcols[:, MASKC:MASKC + 1], scalar2=None, op0=ALU.mult),
                                reads=[("abuf", l, c), "pcols"], writes=[("abuf", l, c)])
                        else:
                            S.pool(lambda e, l=l, c=c, Tp=Tp: e.tensor_copy(
                                out=abuf[:, l, c, 0:30], in_=abuf[:, l, c, Tp:Tp + 30]),
                                reads=[("abuf", l, c)], writes=[("abuf", l, c)])
                    S.dve(lambda e, ba=ba, r=r, l=l, c=c, T=T: e.tensor_tensor(
                        out=abuf[:, l, c, 30:30 + T], in0=psum[ba][:, 0:T], in1=sg[:, r, 0:T], op=ALU.mult),
                        reads=[PK(ba), ("sg", r)], writes=[("abuf", l, c)])
                    bfree(ba)
                if l == 0 and pending[0] is not None:
                    pending[0]()
                    pending[0] = None
                lna_run = ln_begin(3, T, 3)
                for c in range(3):
                    b = balloc()
                    wl = []
                    for k in range(CK):
                        idx = c * CK + k
                        wap, wkey = W(pkey, l, SB_DIAG + idx // 8)
                        wl.append((wap[:, (idx % 8) * 128:(idx % 8 + 1) * 128], wkey))

                    def fn(e, b=b, c=c, wl=wl, l=l, T=T):
                        for k in range(CK):
                            ins = e.matmul(psum[b][:, 0:T], lhsT=wl[k][0], rhs=abuf[:, l, c, k:k + T],
                                           start=(k == 0), stop=(k == CK - 1))
                        return ins
                    S.pe(fn, reads=list(set(w[1] for w in wl)) + [("abuf", l, c)], writes=[PK(b)])
                    S.act(lambda e, b=b, c=c, l=l, T=T: e.activation(
                        out=yc[:, c, 0:T], in_=psum[b][:, 0:T], func=AF.Identity, bias=col(l, O_CAB, c), scale=1.0),
                        reads=[PK(b), "pcols"], writes=[("yc", c)])
                    bfree(b)
                    ln_feed(lna_run, yc[:, c, 0:T], ("yc", c))
                zc_ = [yc[:, c, 0:T] for c in range(3)]
                zk_ = [("yc", c) for c in range(3)]
                for tc in range(nt):
                    b = balloc()
                    waps = [W(pkey, l, SB_V + c) for c in range(3)]

                    def fn(e, tc=tc, b=b, waps=waps):
                        for c in range(3):
                            for kc in range(8):
                                ins = e.matmul(psum[b][:, c * 128:(c + 1) * 128],
                                               lhsT=xb[:, kc, tc * 128:(tc + 1) * 128],
                                               rhs=waps[c][0][:, kc * 128:(kc + 1) * 128],
                                               start=(kc == 0), stop=(kc == 7))
                        return ins
                    S.pe(fn, reads=[w[1] for w in waps] + xbk, writes=[PK(b)])
                    vflat = vg[:, tc].rearrange("p j r d -> p (j r d)")
                    S.act(lambda e, b=b, vflat=vflat: e.activation(out=vflat, in_=psum[b][:, 0:CW], func=AF.Gelu_apprx_tanh),
                          reads=[PK(b)], writes=[("vg", tc)])
                    bfree(b)
                    S.dve(lambda e, tc=tc, vflat=vflat: e.bn_stats(out=vst[:, tc, :], in_=vflat),
                          reads=[("vg", tc)], writes=[("vst", tc)])
                    S.dve(lambda e, tc=tc: e.bn_aggr(out=vmv[:, tc, :], in_=vst[:, tc, :]),
                          reads=[("vst", tc)], writes=["vmv"])

                for c in range(2):
                    b = proj(SB_Q + c)
                    S.act(lambda e, b=b, c=c, T=T: e.activation(out=qT[:, c, 0:T], in_=psum[b][:, 0:T], func=AF.Copy),
                          reads=[PK(b)], writes=[("qT", c)])
                    bfree(b)
                S.act(lambda e, nt=nt: e.activation(out=vrs[:, 0:nt], in_=vmv[:, 0:nt, 1], func=AF.Sqrt, bias=EPS, scale=1.0),
                      reads=["vmv"], writes=["vrs"])
                S.dve(lambda e, nt=nt: e.reciprocal(out=vrs[:, 0:nt], in_=vrs[:, 0:nt]), reads=["vrs"], writes=["vrs"])
                for tc in range(nt):
                    base = vpadb[:, tc, :, :, 0:64]
                    nap = [list(x) for x in base.ap]
                    nap[2][0] = 192
                    pview = bass.AP(base.tensor, base.offset, nap)
                    S.dve(lambda e, tc=tc, pview=pview: e.tensor_scalar(
                        out=pview, in0=vg[:, tc], scalar1=vmv[:, tc, 0:1], scalar2=vrs[:, tc:tc + 1],
                        op0=ALU.subtract, op1=ALU.mult), reads=[("vg", tc), "vmv", "vrs"], writes=[("vpadb", tc)])
                for h in range(4):
                    r = h % 2
                    for mc in range(2):
                        b = balloc()
                        S.pe(lambda e, b=b, h=h, r=r, mc=mc, l=l, T=T: e.matmul(
                            psum[b][:, 0:T], lhsT=kT[r * 64:(r + 1) * 64, l, h // 2, mc * 128:(mc + 1) * 128],
                            rhs=qT[r * 64:(r + 1) * 64, h // 2, 0:T], start=True, stop=True),
                            reads=["kT", ("qT", h // 2)], writes=[PK(b)])
                        S.act(lambda e, b=b, h=h, mc=mc, T=T: e.activation(
                            out=hid[:, h * 2 + mc, 0:T], in_=psum[b][:, 0:T], func=AF.Exp, scale=0.125),
                            reads=[PK(b)], writes=[("hid", h * 2 + mc)])
                        bfree(b)
                for c in range(3):
                    b = proj(SB_U + c)
                    S.act(lambda e, b=b, c=c, T=T: e.activation(out=ub[:, c, 0:T], in_=psum[b][:, 0:T], func=AF.Gelu_apprx_tanh),
                          reads=[PK(b)], writes=[("ub", c)])
                    bfree(b)
                for j in range(3):
                    b = balloc()

                    def fn(e, b=b, j=j, l=l, nt=nt):
                        for tc in range(nt):
                            e.matmul(psum[b][:, tc * 128:(tc + 1) * 128], lhsT=vpadb[:, tc, j, 0, :],
                                     rhs=wsT[:, l, 2 * j, :], start=True, stop=False)
                            ins = e.matmul(psum[b][:, tc * 128:(tc + 1) * 128], lhsT=vpadb[:, tc, j, 1, :],
                                           rhs=wsT[:, l, 2 * j + 1, :], start=False, stop=True)
                        return ins
                    S.pe(fn, reads=[("vpadb", tc) for tc in range(nt)] + ["wsT"], writes=[PK(b)])
                    for tc in range(nt):
                        S.dve(lambda e, b=b, j=j, l=l, tc=tc: e.scalar_tensor_tensor(
                            out=t1g[:, j % 2, tc * 128:(tc + 1) * 128], in0=psum[b][:, tc * 128:(tc + 1) * 128],
                            scalar=col(l, O_LVG, j), in1=bsb[:, l, j, :], op0=ALU.mult, op1=ALU.add),
                            reads=[PK(b), "bsb", "pcols"], writes=[("t1g", j % 2, tc)])
                    bfree(b)
                    S.pool(lambda e, j=j, T=T: e.tensor_tensor(
                        out=cat[:, 3 + j, 0:T], in0=t1g[:, j % 2, 0:T], in1=ub[:, j, 0:T], op=ALU.mult),
                        reads=[("t1g", j % 2, tc) for tc in range(nt)] + [("ub", j)], writes=[("cat", 3 + j)])
                for j in range(2):
                    bo = balloc()
                    bd = balloc()
                    pk = [("hid", (2 * j + r) * 2 + mc) for r in range(2) for mc in range(2)]

                    def fno(e, bo=bo, j=j, l=l, T=T):
                        i = 0
                        for r in range(2):
                            for mc in range(2):
                                ins = e.matmul(psum[bo][:, 0:T], lhsT=vpad[:, l, mc, 2 * j + r, :],
                                               rhs=hid[:, (2 * j + r) * 2 + mc, 0:T], start=(i == 0), stop=(i == 3))
                                i += 1
                        return ins

                    def fnd(e, bd=bd, j=j, T=T):
                        i = 0
                        for r in range(2):
                            for mc in range(2):
                                ins = e.matmul(psum[bd][:, 0:T], lhsT=eones[:, r, :],
                                               rhs=hid[:, (2 * j + r) * 2 + mc, 0:T], start=(i == 0), stop=(i == 3))
                                i += 1
                        return ins
                    S.pe(fnd, reads=pk + ["eones"], writes=[PK(bd)])
                    S.pe(fno, reads=pk + ["vpad"], writes=[PK(bo)])
                    S.dve(lambda e, bd=bd, j=j, T=T: e.reciprocal(out=rs[:, j, 0:T], in_=psum[bd][:, 0:T]),
                          reads=[PK(bd)], writes=[("rs", j)])
                    bfree(bd)
                    S.dve(lambda e, bo=bo, j=j, T=T: e.tensor_tensor(
                        out=cat[:, 6 + j, 0:T], in0=psum[bo][:, 0:T], in1=rs[:, j, 0:T], op=ALU.mult),
                        reads=[PK(bo), ("rs", j)], writes=[("cat", 6 + j)])
                    bfree(bo)
                lna_st = ln_flush(lna_run)
                ln1_run = ln_begin(8, T, 2)
                NG = 6
                wo = [W(pkey, l, SB_OUT + fo) for fo in range(NG)]
                wob = [balloc() for _ in range(NG)]
                for fo in range(NG):
                    def fn1(e, b=wob[fo], wap=wo[fo][0], T=T):
                        for i, kc in enumerate([3, 4, 5, 6, 7]):
                            ins = e.matmul(psum[b][:, 0:T], lhsT=wap[:, kc * 128:(kc + 1) * 128],
                                           rhs=cat[:, kc, 0:T], start=(i == 0), stop=False)
                        return ins
                    S.pe(fn1, reads=[wo[fo][1]] + [("cat", k) for k in range(3, 8)], writes=[PK(wob[fo])])
                ln_apply(lna_st, zc_, zk_, 3, CW, T,
                         [col(l, O_LAG, c) for c in range(3)], [col(l, O_LAB, c) for c in range(3)], AF.Silu,
                         [[(cat[:, c, 0:T], ("cat", c))] for c in range(3)])
                def wout_evac(b, fo, T=T):
                    S.dve(lambda e: e.scalar_tensor_tensor(
                        out=xf[:, fo, 0:T], in0=xf[:, fo, 0:T], scalar=ALPHA, in1=psum[b][:, 0:T],
                        op0=ALU.mult, op1=ALU.add), reads=[PK(b), ("xf", fo)], writes=[("xf", fo)])
                    bfree(b)
                    ln_feed(ln1_run, xf[:, fo, 0:T], ("xf", fo))
                for fo in range(NG):
                    def fn2(e, b=wob[fo], wap=wo[fo][0], T=T):
                        for i, kc in enumerate([0, 1, 2]):
                            ins = e.matmul(psum[b][:, 0:T], lhsT=wap[:, kc * 128:(kc + 1) * 128],
                                           rhs=cat[:, kc, 0:T], start=False, stop=(i == 2))
                        return ins
                    S.pe(fn2, reads=[wo[fo][1]] + [("cat", k) for k in range(3)], writes=[PK(wob[fo])])
                    wout_evac(wob[fo], fo)
                for fo in range(NG, 8):
                    wap, wkey = W(pkey, l, SB_OUT + fo)
                    b = balloc()

                    def fn(e, b=b, wap=wap, T=T):
                        for kc in range(8):
                            ins = e.matmul(psum[b][:, 0:T], lhsT=wap[:, kc * 128:(kc + 1) * 128],
                                           rhs=cat[:, kc, 0:T], start=(kc == 0), stop=(kc == 7))
                        return ins
                    S.pe(fn, reads=[wkey] + [("cat", k) for k in range(8)], writes=[PK(b)])
                    wout_evac(b, fo)
                ln_apply(ln_flush(ln1_run), [xf[:, k, 0:T] for k in range(8)], [("xf", k) for k in range(8)], 8, D, T,
                      [col(l, O_L1G, k) for k in range(8)], [col(l, O_L1B, k) for k in range(8)], AF.Identity,
                      [[(xb[:, k, 0:T], ("xb", k)), (xf[:, k, 0:T], ("xf", k))] for k in range(8)])
                ffn_first = proj_multi([SB_UP + i_ for i_ in range(6)])
                for j in range(NJ):
                    r = j % 2
                    if j < 3:
                        bg, bv = ffn_first[2 * j], ffn_first[2 * j + 1]
                    else:
                        bg = proj(SB_UP + 2 * j)
                        bv = proj(SB_UP + 2 * j + 1)
                    if ti == 1:
                        S.pool(lambda e, r=r, l=l, j=j: e.tensor_scalar(
                            out=gb[:, r, 0:2], in0=gst[:, l, j, :], scalar1=pcols[:, MASKC:MASKC + 1],
                            scalar2=None, op0=ALU.mult), reads=[("gst", l), "pcols"], writes=[("gb", r)])
                    else:
                        S.pool(lambda e, r=r, l=l, j=j: e.tensor_copy(out=gb[:, r, 0:2], in_=gst[:, l, j, :]),
                               reads=[("gst", l)], writes=[("gb", r)])
                    S.act(lambda e, bg=bg, r=r, T=T: e.activation(out=gb[:, r, 2:2 + T], in_=psum[bg][:, 0:T], func=AF.Copy),
                          reads=[PK(bg)], writes=[("gb", r)])
                    S.act(lambda e, bg=bg, r=r, T=T, l=l, j=j: e.activation(
                        out=ft1[:, r, 0:T], in_=psum[bg][:, 0:T], func=AF.Identity,
                        bias=col(l, O_CFB, j), scale=col(l, O_CFW, 2 * NJ + j)),
                        reads=[PK(bg), "pcols"], writes=[("ft1", r)])
                    bfree(bg)
                    S.pool(lambda e, r=r, l=l, j=j, T=T: e.tensor_copy(out=gst[:, l, j, :], in_=gb[:, r, T:T + 2]),
                           reads=[("gb", r)], writes=[("gst", l)])
                    S.dve(lambda e, r=r, l=l, j=j, T=T: e.scalar_tensor_tensor(
                        out=facc[:, r, 0:T], in0=gb[:, r, 1:1 + T], scalar=col(l, O_CFW, NJ + j),
                        in1=ft1[:, r, 0:T], op0=ALU.mult, op1=ALU.add),
                        reads=[("gb", r), ("ft1", r), "pcols"], writes=[("facc", r)])
                    S.dve(lambda e, r=r, l=l, j=j, T=T: e.scalar_tensor_tensor(
                        out=facc[:, r, 0:T], in0=gb[:, r, 0:T], scalar=col(l, O_CFW, j),
                        in1=facc[:, r, 0:T], op0=ALU.mult, op1=ALU.add),
                        reads=[("gb", r), ("facc", r), "pcols"], writes=[("facc", r)])
                    S.act(lambda e, r=r, T=T: e.activation(out=fs[:, r, 0:T], in_=facc[:, r, 0:T], func=AF.Silu),
                          reads=[("facc", r)], writes=[("fs", r)])
                    S.dve(lambda e, r=r, j=j, bv=bv, T=T: e.tensor_tensor(
                        out=hid[:, j, 0:T], in0=psum[bv][:, 0:T], in1=fs[:, r, 0:T], op=ALU.mult),
                        reads=[PK(bv), ("fs", r)], writes=[("hid", j)])
                    bfree(bv)
                if l == L - 1 and ti + 1 < len(tiles):
                    load_xb(ti + 1)
                if ti == 0 and l == L - 1:
                    ensure_produced(l, NUNIT - 1)
                    load_xf(ti + 1)
                else:
                    ln2_run = ln_begin(8, T, 1)
                    for hf in range(2):
                        for f4 in range(4):
                            fo = hf * 4 + f4
                            a_ = balloc()
                            for j in range(NJ):
                                wap, wkey = W(pkey, l, SB_DOWN + hf * 11 + j // 2)
                                wv = wap[:, (j % 2) * 512 + f4 * 128:(j % 2) * 512 + (f4 + 1) * 128]
                                S.pe(lambda e, wv=wv, j=j, a_=a_, T=T: e.matmul(
                                    psum[a_][:, 0:T], lhsT=wv, rhs=hid[:, j, 0:T], start=(j == 0), stop=(j == NJ - 1)),
                                    reads=[wkey, ("hid", j)], writes=[PK(a_)])
                            S.dve(lambda e, a_=a_, fo=fo, T=T: e.scalar_tensor_tensor(
                                out=xf[:, fo, 0:T], in0=xf[:, fo, 0:T], scalar=ALPHA, in1=psum[a_][:, 0:T],
                                op0=ALU.mult, op1=ALU.add), reads=[PK(a_), ("xf", fo)], writes=[("xf", fo)])
                            bfree(a_)
                            ln_feed(ln2_run, xf[:, fo, 0:T], ("xf", fo))
                    last = (l == L - 1)
                    outs = [([] if last else [(xb[:, k, 0:T], ("xb", k))]) + [(xf[:, k, 0:T], ("xf", k))] for k in range(8)]
                    zc2 = [xf[:, k, 0:T] for k in range(8)]
                    zk2 = [("xf", k) for k in range(8)]
                    st2 = ln_flush(ln2_run)

                    def fin(st2=st2, zc2=zc2, zk2=zk2, T=T, l=l, outs=outs, last=last, ti=ti, c0=c0):
                        ln_apply(st2, zc2, zk2, 8, D, T, [col(l, O_L2G, k) for k in range(8)],
                                 [col(l, O_L2B, k) for k in range(8)], AF.Identity, outs)
                        if last:
                            if ti >= 1:
                                dst = outT[:, c0 - H:c0 - H + T].rearrange("(kc p) t -> p kc t", p=128)
                                out_ops.append(S.dma(lambda e, dst=dst, T=T: e.dma_start(out=dst, in_=xf[:, :, 0:T]), "xst",
                                                     reads=[("xf", k) for k in range(8)], eng="act"))
                            if ti + 1 < len(tiles):
                                load_xf(ti + 1)
                    if last and ti + 1 < len(tiles):
                        pending[0] = fin
                    else:
                        fin()
        S.finish(final_waits=out_ops)
    return nc


def _pcols(L, params, mask):
    (conv_a_w, conv_a_b, ln_a_g, ln_a_b, ln1_g, ln1_b, ln2_g, ln2_b, conv_f_w, conv_f_b, ln_v_g, ln_v_b) = params
    pc = np.zeros((128, L * PL + 1), np.float32)
    for l in range(L):
        o = l * PL
        pc[:, o + O_CAW:o + O_CAW + 93] = conv_a_w[l].reshape(CK, 3, 128).transpose(2, 1, 0).reshape(128, 93)
        pc[:, o + O_CAB:o + O_CAB + 3] = conv_a_b[l].reshape(3, 128).T
        pc[:, o + O_LAG:o + O_LAG + 3] = ln_a_g[l].reshape(3, 128).T
        pc[:, o + O_LAB:o + O_LAB + 3] = ln_a_b[l].reshape(3, 128).T
        pc[:, o + O_L1G:o + O_L1G + 8] = ln1_g[l].reshape(8, 128).T
        pc[:, o + O_L1B:o + O_L1B + 8] = ln1_b[l].reshape(8, 128).T
        pc[:, o + O_L2G:o + O_L2G + 8] = ln2_g[l].reshape(8, 128).T
        pc[:, o + O_L2B:o + O_L2B + 8] = ln2_b[l].reshape(8, 128).T
        cfw = np.zeros((3, NJ * 128), np.float32)
        cfw[:, :DFF] = conv_f_w[l]
        pc[:, o + O_CFW:o + O_CFW + 66] = cfw.reshape(3, NJ, 128).transpose(2, 0, 1).reshape(128, 66)
        cfb = np.zeros((NJ * 128,), np.float32)
        cfb[:DFF] = conv_f_b[l]
        pc[:, o + O_CFB:o + O_CFB + NJ] = cfb.reshape(NJ, 128).T
        pc[:, o + O_LVG:o + O_LVG + 3] = ln_v_g[l].reshape(3, 128).T
        pc[:, o + O_LVB:o + O_LVB + 3] = ln_v_b[l].reshape(3, 128).T
    pc[:, L * PL] = mask
    return pc


_NC_CACHE = {}


def _get_nc(L, H, n_main):
    key = (L, H, n_main)
    if key not in _NC_CACHE:
        _NC_CACHE[key] = build(L, H, n_main)
    return _NC_CACHE[key]


def _run(x, mem, layers, P, H, n_main=8):
    L = len(layers)
    sl = lambda a: np.ascontiguousarray(a[layers])
    nc = _get_nc(L, H, n_main)
    ident = np.eye(128, dtype=np.float32)
    triu = np.triu(np.ones((128, 128), np.float32))
    bs = sl(P["b_s"])
    bsb = np.ascontiguousarray(
        np.repeat(bs.reshape(L, 3, 2, 1, 128), 64, axis=3).transpose(2, 3, 0, 1, 4).reshape(128, L * 3 * 128))
    wsT = np.ascontiguousarray(sl(P["w_s"]).transpose(0, 1, 3, 2))
    params = tuple(sl(P[k]) for k in ("conv_a_w", "conv_a_b", "ln_a_g", "ln_a_b", "ln1_g", "ln1_b",
                                      "ln2_g", "ln2_b", "conv_f_w", "conv_f_b", "ln_v_g", "ln_v_b"))
    shared = dict(ident=ident, triu=triu, bsb=bsb, wsT=wsT,
                  w_in=sl(P["w_in"]), w_out=sl(P["w_out"]), w_up=sl(P["w_up"]), w_down=sl(P["w_down"]),
                  w_mk=sl(P["w_mk"]), w_mv=sl(P["w_mv"]))
    ntok = n_main * TW
    in_maps = []
    for c in range(NCORES):
        b = c // 4
        t0 = (c % 4) * TOK_PER_CORE
        xt = np.zeros((D, H + ntok), np.float32)
        lo = t0 - H
        if lo >= 0:
            xt[:, :] = x[b, lo:t0 + ntok, :].T
        else:
            xt[:, H:] = x[b, t0:t0 + ntok, :].T
        m = dict(shared)
        m["xT"] = xt
        m["memT"] = np.ascontiguousarray(mem[b].T)
        m["pcols"] = _pcols(L, params, 0.0 if t0 == 0 else 1.0)
        in_maps.append(m)
    res = run_bass_kernel_spmd(nc, in_maps, core_ids=list(range(NCORES)))
    out = np.empty((BATCH, SEQ, D), np.float32)
    for c in range(NCORES):
        b = c // 4
        t0 = (c % 4) * TOK_PER_CORE
        out[b, t0:t0 + ntok, :] = res.results[c]["outT"].T
    return out


FUSED = True


def kernel(x, mem, **P):
    x = np.asarray(x, np.float32)
    mem = np.asarray(mem, np.float32)
    P = {k: np.asarray(v, np.float32) for k, v in P.items()}
    if FUSED:
        return _run(x, mem, [0, 1], P, H=256)
    for l in range(DEPTH):
        x = _run(x, mem, [l], P, H=128)
    return x
```
